# Optimizing a Trainium2 kernel written in Bass

```python
import math
import jax, jax.numpy as jnp
from jax import lax
import numpy as np

D_MODEL = 1024
BATCH = 8
SEQ = 2048
DEPTH = 1

GLA_HEADS = 4
GLA_DK = D_MODEL // 2
GLA_DV = D_MODEL
GLA_HK = GLA_DK // GLA_HEADS
GLA_HV = GLA_DV // GLA_HEADS
GLA_RANK = 16
GLA_TAU = 16.0
GLA_CHUNK = 64
SB_HEADS = 8
SB_WIDTH = D_MODEL
SB_HD = SB_WIDTH // SB_HEADS
SB_BLOCK = 128
N_BRANCH = 2
EPS = 1e-6

SPLIT_SIZES = [GLA_DK, GLA_DK, GLA_DV, GLA_DV, GLA_RANK,
               SB_WIDTH, SB_WIDTH, SB_WIDTH, SB_WIDTH, N_BRANCH * D_MODEL]
SPLIT_IDX = [int(v) for v in np.cumsum(SPLIT_SIZES)[:-1]]
IN_COLS = int(sum(SPLIT_SIZES))

kernel_name = "gla_stickbreaking_gated_hybrid"


def rmsnorm(x, g):
    xf = x.astype(jnp.float32)
    y = xf * lax.rsqrt(jnp.mean(xf * xf, axis=-1, keepdims=True) + EPS)
    return y.astype(x.dtype) * g


def gla_chunked(q, k, v, log_a):
    B, T, H, dk = q.shape
    dv = v.shape[-1]
    C = GLA_CHUNK
    n = T // C
    def to_chunks(a):
        return a.reshape(B, n, C, H, a.shape[-1]).transpose(0, 3, 1, 2, 4).astype(jnp.float32)
    qc, kc, vc, la = to_chunks(q), to_chunks(k), to_chunks(v), to_chunks(log_a)
    bcum = jnp.cumsum(la, axis=3)
    b_last = bcum[:, :, :, -1:, :]
    qe = qc * jnp.exp(bcum) * (dk ** -0.5)
    ke = kc * jnp.exp(-bcum)
    kd = kc * jnp.exp(b_last - bcum)
    mask = jnp.tril(jnp.ones((C, C), jnp.float32))
    attn = jnp.einsum('bhnid,bhnjd->bhnij', qe, ke) * mask
    o_intra = jnp.einsum('bhnij,bhnjv->bhniv', attn, vc)

    def step(S, inp):
        q_i, k_i, v_i, d_i = inp
        o = jnp.einsum('bhid,bhdv->bhiv', q_i, S)
        S = d_i[..., None] * S + jnp.einsum('bhjd,bhjv->bhdv', k_i, v_i)
        return S, o

    xs = (jnp.moveaxis(qe, 2, 0), jnp.moveaxis(kd, 2, 0), jnp.moveaxis(vc, 2, 0),
          jnp.moveaxis(jnp.exp(b_last[:, :, :, 0, :]), 2, 0))
    S0 = jnp.zeros((B, H, dk, dv), jnp.float32)
    _, o_inter = lax.scan(step, S0, xs)
    o = o_intra + jnp.moveaxis(o_inter, 0, 2)
    return o.transpose(0, 2, 3, 1, 4).reshape(B, T, H, dv)


def stick_breaking(q, k, v):
    B, T, H, d = q.shape
    qh = q.transpose(0, 2, 1, 3)
    kh = k.transpose(0, 2, 1, 3)
    vh = v.transpose(0, 2, 1, 3)
    scale = 1.0 / math.sqrt(d)
    outs = []
    for i in range(T // SB_BLOCK):
        L = (i + 1) * SB_BLOCK
        qb = qh[:, :, i * SB_BLOCK:L]
        z = jnp.einsum('bhqd,bhkd->bhqk', qb, kh[:, :, :L]).astype(jnp.float32) * scale
        tpos = i * SB_BLOCK + jnp.arange(SB_BLOCK)[:, None]
        spos = jnp.arange(L)[None, :]
        causal = spos < tpos
        log1m = jnp.where(causal, jax.nn.log_sigmoid(-z), 0.0)
        between = lax.cumsum(log1m, axis=3, reverse=True) - log1m
        A = jnp.where(causal, jnp.exp(jax.nn.log_sigmoid(z) + between), 0.0)
        outs.append(jnp.einsum('bhqk,bhkd->bhqd', A.astype(vh.dtype), vh[:, :, :L]))
    o = jnp.concatenate(outs, axis=2)
    return o.transpose(0, 2, 1, 3)


def setup_inputs(seed: int = 0) -> dict:
    key = jax.random.key(seed)
    ks = jax.random.split(key, 12)
    f = jnp.float32
    x = jax.random.normal(ks[0], (BATCH, SEQ, D_MODEL), f)
    norm_g = 1.0 + 0.02 * jax.random.normal(ks[1], (D_MODEL,), f)
    w_in = jax.random.normal(ks[2], (D_MODEL, IN_COLS), f) * D_MODEL ** -0.5
    w_dec_up = jax.random.normal(ks[3], (GLA_RANK, GLA_DK), f) * GLA_RANK ** -0.5
    b_dec = 0.1 * jax.random.normal(ks[4], (GLA_DK,), f)
    gla_norm_g = 1.0 + 0.02 * jax.random.normal(ks[5], (GLA_HV,), f)
    w_pa = jax.random.normal(ks[6], (GLA_DV, D_MODEL), f) * GLA_DV ** -0.5
    w_pb = jax.random.normal(ks[7], (SB_WIDTH, D_MODEL), f) * SB_WIDTH ** -0.5
    b_gate = 0.01 * jax.random.normal(ks[8], (N_BRANCH * D_MODEL,), f)
    w_o = jax.random.normal(ks[9], (D_MODEL, D_MODEL), f) * D_MODEL ** -0.5
    final_g = 1.0 + 0.02 * jax.random.normal(ks[10], (D_MODEL,), f)
    return {"x": x, "norm_g": norm_g, "w_in": w_in, "w_dec_up": w_dec_up, "b_dec": b_dec,
            "gla_norm_g": gla_norm_g, "w_pa": w_pa, "w_pb": w_pb, "b_gate": b_gate,
            "w_o": w_o, "final_g": final_g}


def reference(x, norm_g, w_in, w_dec_up, b_dec, gla_norm_g, w_pa, w_pb, b_gate, w_o, final_g):
    B, T, _ = x.shape
    for _layer in range(DEPTH):
        h = rmsnorm(x, norm_g)
        proj = h @ w_in
        (g_q, g_k, g_v, g_gate, g_rank,
         s_q, s_k, s_v, s_gate, m_logits) = jnp.split(proj, SPLIT_IDX, axis=-1)

        log_a = jax.nn.log_sigmoid((g_rank @ w_dec_up + b_dec).astype(jnp.float32)) / GLA_TAU
        o_gla = gla_chunked(g_q.reshape(B, T, GLA_HEADS, GLA_HK),
                            g_k.reshape(B, T, GLA_HEADS, GLA_HK),
                            g_v.reshape(B, T, GLA_HEADS, GLA_HV),
                            log_a.reshape(B, T, GLA_HEADS, GLA_HK)).astype(x.dtype)
        o_gla = rmsnorm(o_gla, gla_norm_g).reshape(B, T, GLA_DV) * jax.nn.silu(g_gate)
        y_a = o_gla @ w_pa

        o_sb = stick_breaking(s_q.reshape(B, T, SB_HEADS, SB_HD),
                              s_k.reshape(B, T, SB_HEADS, SB_HD),
                              s_v.reshape(B, T, SB_HEADS, SB_HD))
        o_sb = o_sb.reshape(B, T, SB_WIDTH) * jax.nn.silu(s_gate)
        y_b = o_sb @ w_pb

        gates = jax.nn.sigmoid(m_logits + b_gate).reshape(B, T, N_BRANCH, D_MODEL)
        merged = gates[:, :, 0] * y_a + gates[:, :, 1] * y_b
        x = x + merged @ w_o
    return rmsnorm(x, final_g)
```

```python
import os
import numpy as np
import concourse.bass as bass
import concourse.mybir as mybir
from concourse.bass_utils import run_bass_kernel_spmd

F32 = mybir.dt.float32
F32R = mybir.dt.float32r
BF16 = mybir.dt.bfloat16
AF = mybir.ActivationFunctionType
ALU = mybir.AluOpType

D = 1024
T = 2048
NT = 16
C_GQ, C_GK, C_GV, C_GG, C_GR = 0, 512, 1024, 2048, 3072
C_SQ, C_SK, C_SV, C_SG, C_ML = 3088, 4112, 5136, 6160, 7184
NCOL = 9232
EPS = 1e-6
ENGS = ("pe", "act", "dve", "pool", "sp")
SEM_ROT = 2000


class Buf:
    __slots__ = ("name", "w", "r", "dsem")

    def __init__(self, name):
        self.name = name
        self.w = None
        self.r = {}
        self.dsem = None


class Prog:
    def __init__(self, nc):
        self.nc = nc
        self.q = {e: [] for e in ENGS}
        self.cnt = {}
        self.sem = {}
        self.cur = {}
        self.waited = {e: {} for e in ENGS}
        self._ctx = []
        self.nsem = 0
        for e in ENGS:
            self.cur[e] = self._new_sem(e)

    def _new_sem(self, tag):
        key = "%s_%d" % (tag, self.nsem)
        self.nsem += 1
        cm = self.nc.semaphore("s_" + key)
        s = cm.__enter__()
        self._ctx.append(cm)
        self.sem[key] = s
        self.cnt[key] = 0
        return key

    def _mkwaits(self, eng, deps):
        waits = []
        for d in deps:
            if d is None:
                continue
            src, val = d
            if self.waited[eng].get(src, 0) >= val:
                continue
            self.waited[eng][src] = val
            waits.append((src, val))
        return waits

    def _deps(self, reads, writes, extra):
        deps = list(extra)
        for b in reads:
            deps.append(b.w)
        for b in writes:
            deps.append(b.w)
            deps.extend(b.r.values())
        return deps

    def _mark(self, tok, reads, writes):
        for b in reads:
            b.r[tok[0]] = tok
        for b in writes:
            b.w = tok
            b.r = {}

    def op(self, eng, fn, reads=(), writes=(), deps=()):
        waits = self._mkwaits(eng, self._deps(reads, writes, deps))
        key = self.cur[eng]
        if self.cnt[key] >= SEM_ROT:
            key = self.cur[eng] = self._new_sem(eng)
        self.cnt[key] += 1
        tok = (key, self.cnt[key])
        self.q[eng].append((waits, fn, key, 1))
        self._mark(tok, reads, writes)
        return tok

    def dma(self, eng, fn, reads=(), writes=(), sembuf=None, deps=()):
        waits = self._mkwaits(eng, self._deps(reads, writes, deps))
        b = sembuf
        if b.dsem is None:
            b.dsem = self._new_sem("d")
        key = b.dsem
        self.cnt[key] += 16
        tok = (key, self.cnt[key])
        self.q[eng].append((waits, fn, key, 16))
        self._mark(tok, reads, writes)
        return tok

    def wait(self, eng, deps):
        waits = self._mkwaits(eng, deps)
        if waits:
            self.q[eng].append((waits, None, None, 0))

    def barrier(self):
        toks = []
        for e in ENGS:
            key = self.cur[e]
            if self.cnt[key] > 0:
                toks.append((key, self.cnt[key]))
        for e in ENGS:
            self.wait(e, toks)

    def emit(self):
        nc = self.nc
        P = self
        with nc.Block() as block:
            def run(engname):
                def body(engine):
                    for waits, fn, sk, inc in P.q[engname]:
                        for src, val in waits:
                            engine.wait_ge(P.sem[src], val)
                        if fn is None:
                            continue
                        inst = fn(engine)
                        if sk is not None:
                            inst.then_inc(P.sem[sk], inc)
                return body
            block.tensor(run("pe"))
            block.scalar(run("act"))
            block.vector(run("dve"))
            block.gpsimd(run("pool"))
            block.sync(run("sp"))

    def close(self):
        for cm in reversed(self._ctx):
            cm.__exit__(None, None, None)
        self._ctx = []


class Alloc:
    def __init__(self, nc):
        self.nc = nc
        self._ctx = []

    def sb(self, name, shape, dt):
        cm = self.nc.sbuf_tensor(name, list(shape), dt)
        t = cm.__enter__()
        self._ctx.append(cm)
        return t

    def ps(self, name, shape, dt):
        cm = self.nc.psum_tensor(name, list(shape), dt)
        t = cm.__enter__()
        self._ctx.append(cm)
        return t

    def close(self):
        for cm in reversed(self._ctx):
            cm.__exit__(None, None, None)
        self._ctx = []


def host_consts():
    c = np.zeros((128, 4, 128), np.float32)
    idx = np.arange(128)
    c[:, 0, :] = np.eye(128, dtype=np.float32)
    j = idx[:, None]
    i = idx[None, :]
    c[:, 1, :] = (i >= j).astype(np.float32)
    c[:, 2, :] = -(j >= i).astype(np.float32)
    c[:, 3, :] = np.where(j >= i, -30000.0, 0.0).astype(np.float32)
    return c


def build_nc(debug=None, stop_after=None):
    nc = bass.Bass("TRN2", target_bir_lowering=False)

    def din(name, shape):
        return nc.dram_tensor(name, list(shape), F32, kind="ExternalInput").ap()

    x_d = din("x", [T, D])
    w_in = din("w_in", [D, NCOL])
    norm_g_d = din("norm_g", [D])
    w_dec_up_d = din("w_dec_up", [16, 512])
    b_dec_d = din("b_dec", [512])
    gla_g_d = din("gla_norm_g", [256])
    w_pa = din("w_pa", [D, D])
    w_pb = din("w_pb", [D, D])
    b_gate_d = din("b_gate", [2 * D])
    w_o = din("w_o", [D, D])
    final_g_d = din("final_g", [D])
    cst_d = din("cst", [128, 4, 128])
    out_d = nc.dram_tensor("out", [T, D], F32, kind="ExternalOutput").ap()
    dbg_d = {}
    if debug:
        for name, shape in debug.items():
            dbg_d[name] = nc.dram_tensor("dbg_" + name, list(shape), F32, kind="ExternalOutput").ap()

    A = Alloc(nc)
    P = Prog(nc)
    out_toks = []

    hT = A.sb("hT", [128, 8, T], BF16)
    og = A.sb("og", [128, 8, T], BF16)
    osb_raw = A.sb("osb", [128, 8192], F32)
    RA = A.sb("RA", [128, 15360], F32)
    fr_a = A.sb("fr_a", [128, 1536], F32)
    fr_R = A.sb("fr_R", [128, 2048], F32)
    zeros = A.sb("zeros", [128, 512], F32)
    NW = 4
    wsl = [A.sb("w%d" % i, [128, 8, 256], BF16) for i in range(NW)]
    wslb = [Buf("w%d" % i) for i in range(NW)]
    cst = A.sb("cst_sb", [128, 4, 128], F32)
    ident16 = A.sb("ident16", [128, 128], BF16)
    negmask16 = A.sb("negmask16", [128, 128], BF16)
    negU = A.sb("negU", [128, 128], F32)
    negones = A.sb("negones", [128, 128], F32)
    ones_r = A.sb("ones_r", [128, 128], F32)
    gvec = A.sb("gvec", [128, 8], F32)
    bgate = A.sb("bgate", [128, 16], F32)
    negbdec = A.sb("negbdec", [128, 4], F32)
    glag = A.sb("glag", [128, 2], F32)
    wdec = A.sb("wdec", [16, 512], F32)
    small = A.sb("small", [128, 128], F32)
    gtmp = A.sb("gtmp", [128, 512], F32)
    gg1 = A.sb("gg1", [128, 2048], F32)
    gsm = A.sb("gsm", [128, 768], F32)
    pb = [A.ps("pb%d" % i, [128, 512], F32) for i in range(8)]
    pbb = [Buf("pb%d" % i) for i in range(8)]

    ident32 = cst[:, 0, :]
    pairmask = cst[:, 1, :]
    osb = osb_raw[:].bitcast(BF16).rearrange("p (k t) -> p k t", k=8)

    b_cst = Buf("cst")
    b_consts = Buf("consts")
    b_hTk = [Buf("hT%d" % k) for k in range(8)]
    b_og = [Buf("og%d" % k) for k in range(8)]
    b_osb = [Buf("osb%d" % k) for k in range(8)]

    def r32(ap):
        return ap.bitcast(F32R)

    b_gv, b_bg, b_nb, b_gl, b_wd = Buf("gv"), Buf("bg"), Buf("nb"), Buf("gl"), Buf("wd")
    P.dma("sp", lambda e: e.dma_start(out=cst[:], in_=cst_d), writes=[b_cst], sembuf=b_cst)
    P.dma("sp", lambda e: e.dma_start(out=gvec[:], in_=norm_g_d.rearrange("(k p) -> p k", p=128),
                                      allow_slow_non_contiguous=True), writes=[b_gv], sembuf=b_gv)
    P.dma("sp", lambda e: e.dma_start(out=bgate[:], in_=b_gate_d.rearrange("(k p) -> p k", p=128),
                                      allow_slow_non_contiguous=True), writes=[b_bg], sembuf=b_bg)
    P.dma("sp", lambda e: e.dma_start(out=negbdec[:], in_=b_dec_d.rearrange("(k p) -> p k", p=128),
                                      allow_slow_non_contiguous=True), writes=[b_nb], sembuf=b_nb)
    P.dma("sp", lambda e: e.dma_start(out=glag[:], in_=gla_g_d.rearrange("(k p) -> p k", p=128),
                                      allow_slow_non_contiguous=True), writes=[b_gl], sembuf=b_gl)
    P.dma("sp", lambda e: e.dma_start(out=wdec[:], in_=w_dec_up_d), writes=[b_wd], sembuf=b_wd)
    P.op("dve", lambda e: e.tensor_copy(out=ident16[:], in_=ident32), reads=[b_cst], writes=[b_consts])
    P.op("dve", lambda e: e.tensor_copy(out=negmask16[:], in_=cst[:, 3, :]), reads=[b_cst], writes=[b_consts])
    P.op("dve", lambda e: e.tensor_copy(out=r32(negU[:]), in_=cst[:, 2, :]), reads=[b_cst], writes=[b_consts])
    P.op("dve", lambda e: e.memset(RA[:, 14848:14976], -1.0), writes=[b_consts])
    P.op("dve", lambda e: e.tensor_copy(out=r32(negones[:]), in_=RA[:, 14848:14976]), writes=[b_consts])
    P.op("dve", lambda e: e.memset(RA[:, 15104:15232], 1.0), writes=[b_consts])
    P.op("dve", lambda e: e.tensor_copy(out=r32(ones_r[:]), in_=RA[:, 15104:15232]), writes=[b_consts])
    P.op("dve", lambda e: e.memset(zeros[:], 0.0), writes=[b_consts])
    P.op("dve", lambda e: e.tensor_scalar(out=negbdec[:], in0=negbdec[:], scalar1=-1.0, scalar2=None,
                                          op0=ALU.mult), reads=[b_nb], writes=[b_consts])

    if stop_after == "consts":
        P.wait("sp", out_toks)
        P.emit(); P.close(); A.close()
        return nc

    wstate = {"i": 0}

    def wload(src, pieces):
        i = wstate["i"] % NW
        wstate["i"] += 1
        t, b = wsl[i], wslb[i]
        offs = []
        o = 0
        for pc in pieces:
            if len(pc) == 3:
                srcp, c0, n = pc
            else:
                srcp = src
                c0, n = pc
            sv = srcp.rearrange("(k p) c -> p k c", p=128)
            P.dma("pool", lambda e, o=o, c0=c0, n=n, t=t, sv=sv: e.dma_start(out=t[:, :, o:o + n], in_=sv[:, :, c0:c0 + n]),
                  writes=[b], sembuf=b)
            offs.append(o)
            o += n
        assert o <= 256
        return t, b, offs

    pbrot = {"i": 0}

    def proj_fm(wt, wb, off, m, rhs_fn, rhs_bufs, evac, banks=(0, 1)):
        for tg in range(4):
            bi = banks[pbrot["i"] % len(banks)]
            pbrot["i"] += 1
            ps, psb = pb[bi], pbb[bi]

            def mm(e, ps=ps, tg=tg):
                last = None
                for k in range(8):
                    last = e.matmul(ps[0:m, :], wt[:, k, off:off + m], rhs_fn(k, tg),
                                    start=(k == 0), stop=(k == 7))
                return last
            P.op("pe", mm, reads=[wb] + list(rhs_bufs), writes=[psb])
            evac(tg, ps, psb)

    def hT_rhs(k, tg):
        return hT[:, k, tg * 512:(tg + 1) * 512]

    NXB = 4
    xt = [RA[:, i * 1024:(i + 1) * 1024] for i in range(NXB)]
    xtb = [Buf("xt%d" % i) for i in range(NXB)]
    junk = RA[:, 4096:5120]
    b_junk = Buf("junk")
    b_small = [Buf("small%d" % i) for i in range(4)]
    def p1_dma(tt):
        xi, xb = xt[tt % NXB], xtb[tt % NXB]
        P.dma("sp", lambda e: e.dma_start(out=xi, in_=x_d[tt * 128:(tt + 1) * 128, :]), writes=[xb], sembuf=xb)

    def p1_A(tt):
        xi, xb = xt[tt % NXB], xtb[tt % NXB]
        sm = small[:, (tt % NXB) * 4:(tt % NXB) * 4 + 4]
        smb = b_small[tt % NXB]
        P.op("act", lambda e: e.activation(out=junk, in_=xi, func=AF.Square, accum_out=sm[:, 0:1]), reads=[xb], writes=[smb])
        P.op("act", lambda e: e.activation(out=sm[:, 1:2], in_=sm[:, 0:1], func=AF.Ln, scale=1.0 / D, bias=EPS),
             reads=[smb], writes=[smb])
        P.op("act", lambda e: e.activation(out=sm[:, 2:3], in_=sm[:, 1:2], func=AF.Exp, scale=-0.5), reads=[smb], writes=[smb])
        P.op("dve", lambda e: e.tensor_scalar(out=xi, in0=xi, scalar1=sm[:, 2:3], scalar2=None, op0=ALU.mult),
             reads=[smb], writes=[xb])

    def p1_B(tt):
        xi, xb = xt[tt % NXB], xtb[tt % NXB]
        for half in range(2):
            bi = (tt * 2 + half) % 4
            ps, psb = pb[bi], pbb[bi]

            def tr(e, ps=ps, half=half):
                last = None
                for q in range(4):
                    k = half * 4 + q
                    last = e.transpose(ps[:, q * 128:(q + 1) * 128], xi[:, k * 128:(k + 1) * 128], ident32)
                return last
            P.op("pe", tr, reads=[xb, b_cst], writes=[psb])
            for q in range(4):
                k = half * 4 + q
                if half == 0:
                    P.op("dve", lambda e, ps=ps, q=q, k=k: e.tensor_scalar(
                        out=hT[:, k, tt * 128:(tt + 1) * 128], in0=ps[:, q * 128:(q + 1) * 128],
                        scalar1=gvec[:, k:k + 1], scalar2=None, op0=ALU.mult), reads=[psb, b_gv], writes=[b_hTk[k]])
                else:
                    P.op("act", lambda e, ps=ps, q=q, k=k: e.activation(
                        out=hT[:, k, tt * 128:(tt + 1) * 128], in_=ps[:, q * 128:(q + 1) * 128],
                        func=AF.Copy, scale=gvec[:, k:k + 1]), reads=[psb, b_gv], writes=[b_hTk[k]])

    for tt in range(NXB):
        p1_dma(tt)
    for step in range(NT + 1):
        if step < NT:
            p1_A(step)
        if step >= 1:
            p1_B(step - 1)
            if step - 1 + NXB < NT:
                p1_dma(step - 1 + NXB)
    P.barrier()

    def dump(name, ap, bufs=()):
        if debug and name in debug:
            out_toks.append(P.dma("sp", lambda e: e.dma_start(out=dbg_d[name], in_=ap), reads=list(bufs), sembuf=Buf("dbg")))

    b_dtmp = Buf("dtmp")

    def dump_chunks(name, src3d, tmp):
        if not (debug and name in debug):
            return
        P.barrier()
        for k in range(8):
            P.op("dve", lambda e, k=k: e.tensor_copy(out=tmp, in_=src3d[:, k, :]), writes=[b_dtmp])
            out_toks.append(P.dma("sp", lambda e, k=k: e.dma_start(out=dbg_d[name][k], in_=tmp), reads=[b_dtmp], sembuf=b_dtmp))
        P.barrier()

    def finish():
        P.barrier()
        P.wait("sp", out_toks)
        P.emit()
        P.close()
        A.close()
        return nc

    dump_chunks("hT", hT, RA[:, 4096:4096 + 2048])
    if stop_after == "hT":
        return finish()

    RB = [Buf("RA%d" % i) for i in range(30)]
    OB = [Buf("OS%d" % i) for i in range(16)]

    def rb(a, b):
        return RB[a // 512:(b + 511) // 512]

    def ob(a, b):
        return OB[a // 512:(b + 511) // 512]

    for k in range(8):
        b_osb[k] = None
    b_osbk = [ob(k * 1024, (k + 1) * 1024) for k in range(8)]

    side_banks = [0, 1, 3]

    def nextbank():
        bi = side_banks[pbrot["i"] % len(side_banks)]
        pbrot["i"] += 1
        return pb[bi], pbb[bi]

    def mm8h(e, ps, m, wt, off, tg, k0=0, k1=8):
        last = None
        for k in range(k0, k1):
            last = e.matmul(ps[0:m, :], wt[:, k, off:off + m], hT[:, k, tg * 512:(tg + 1) * 512], start=(k == 0), stop=(k == 7))
        return last

    def proj_unit(ps, psb, m, wt, wb, off, tg):
        P.op("pe", lambda e: mm8h(e, ps, m, wt, off, tg, 0, 4), reads=[wb] + b_hTk, writes=[psb])
        yield
        P.op("pe", lambda e: mm8h(e, ps, m, wt, off, tg, 4, 8), reads=[wb] + b_hTk, writes=[psb])

    class GSet:
        pass
    gsets = []
    for si in range(2):
        g = GSet()
        base, bf = (osb_raw, ob) if si == 0 else (RA, rb)
        g.qeT = base[:, 0:1024].bitcast(BF16)
        g.keT = base[:, 1024:2048].bitcast(BF16)
        g.ke_tm = base[:, 2048:3072].bitcast(BF16).rearrange("p (n d) -> p n d", n=16)
        g.v_tm = base[:, 3072:5120].bitcast(BF16).rearrange("p (n d) -> p n d", n=16)
        g.b_qeT, g.b_keT, g.b_ketm, g.b_vtm = bf(0, 1024), bf(1024, 2048), bf(2048, 3072), bf(3072, 5120)
        g.eb = small[:, 64 + si * 32:64 + (si + 1) * 32]
        g.b_eb = Buf("eb%d" % si)
        gsets.append(g)
    gg16s = [osb_raw[:, 5120:7168].bitcast(BF16).rearrange("p (v t) -> p v t", v=2),
             gg1[:, :].bitcast(BF16).rearrange("p (v t) -> p v t", v=2)]
    b_ggs = [ob(5120, 7168), [Buf("gg1a"), Buf("gg1b")]]
    rstd_t, b_rstd = osb_raw[:, 7168:7680], ob(7168, 7680)
    tmp_t, b_tmp = osb_raw[:, 7680:8192], ob(7680, 8192)
    o32 = RA[:, 5120:9216].rearrange("p (v t) -> p v t", v=2)
    b_o32 = rb(5120, 9216)
    grT, b_grT = RA[:, 9216:11264], rb(9216, 11264)
    sp_g = [RA[:, 11264 + i * 512:11264 + (i + 1) * 512] for i in range(2)]
    E1_g = [RA[:, 12288 + i * 512:12288 + (i + 1) * 512] for i in range(2)]
    E2_g = [RA[:, 13312 + i * 512:13312 + (i + 1) * 512] for i in range(2)]
    b_spg = [rb(11264 + i * 512, 11264 + (i + 1) * 512) for i in range(2)]
    b_E1g = [rb(12288 + i * 512, 12288 + (i + 1) * 512) for i in range(2)]
    b_E2g = [rb(13312 + i * 512, 13312 + (i + 1) * 512) for i in range(2)]
    rmask, b_rmask = RA[:, 14336:14848], rb(14336, 14848)
    Tst = [RA[:, 14848 + i * 256:14848 + (i + 1) * 256] for i in range(2)]
    b_T = [Buf("T0"), Buf("T1")]
    b_Tblk = rb(14848, 15360)
    S16 = [gsm[:, i * 128:(i + 1) * 128].bitcast(BF16) for i in range(3)]
    at16 = [gsm[:, 384 + i * 64:384 + (i + 1) * 64].bitcast(BF16) for i in range(2)]
    b_S16 = [Buf("S16_%d" % i) for i in range(3)]
    b_at = [Buf("at0"), Buf("at1")]
    b_gtmp = Buf("gtmp")
    sq_t = fr_a[:, 0:1024].rearrange("p (v t) -> p v t", v=2)
    b_sq = Buf("sq")

    P.op("pool", lambda e: e.memset(rmask, 1.0), writes=b_rmask)
    P.op("pool", lambda e: e.memset(rmask.rearrange("p (n c) -> p n c", c=128)[:, :, 0:1], 0.0), writes=b_rmask)

    wt_r, wb_r, offs_r = wload(w_in, [(C_GR, 16)])
    for tg in range(4):
        ps, psb = nextbank()
        P.op("pe", lambda e, ps=ps, tg=tg: mm8h(e, ps, 16, wt_r, offs_r[0], tg), reads=[wb_r] + b_hTk, writes=[psb])
        P.op("dve", lambda e, ps=ps, tg=tg: e.tensor_copy(out=grT[0:16, tg * 512:(tg + 1) * 512], in_=ps[0:16, :]),
             reads=[psb], writes=b_grT)

    def gla_pro(h):
        g = gsets[(h + 1) % 2]
        wt, wb, offs = wload(w_in, [(C_GQ + h * 128, 128), (C_GK + h * 128, 128)])
        wtv, wbv, _ = wload(w_in, [(C_GV + h * 256, 256)])
        wtg, wbg, _ = wload(w_in, [(C_GG + h * 256, 256)])
        gg16, b_gg = gg16s[(h + 1) % 2], b_ggs[(h + 1) % 2]
        b_gt = [b_gtmp]

        def gate_unit(vc, tg):
            ts = slice(tg * 512, (tg + 1) * 512)
            ps, psb = nextbank()
            yield from proj_unit(ps, psb, 128, wtg, wbg, vc * 128, tg)
            P.op("act", lambda e: e.activation(out=gtmp[:, :], in_=ps[:, :], func=AF.Exp, scale=-1.0), reads=[psb], writes=b_gt)
            P.op("act", lambda e: e.activation(out=gtmp[:, :], in_=gtmp[:, :], func=AF.Ln, bias=1.0), reads=b_gt, writes=b_gt)
            P.op("act", lambda e: e.activation(out=gtmp[:, :], in_=gtmp[:, :], func=AF.Exp, scale=-1.0), reads=b_gt, writes=b_gt)
            P.op("dve", lambda e: e.tensor_tensor(out=gg16[:, vc, ts], in0=ps[:, :], in1=gtmp[:, :], op=ALU.mult),
                 reads=[psb] + b_gt, writes=b_gg)
            yield

        for tg in range(4):
            j = tg % 2
            ts = slice(tg * 512, (tg + 1) * 512)
            sp_, E1_, E2_ = sp_g[j], E1_g[j], E2_g[j]
            ps, psb = nextbank()
            P.op("pe", lambda e, ps=ps, ts=ts: e.matmul(ps[:, :], wdec[0:16, h * 128:(h + 1) * 128], grT[0:16, ts],
                                                       start=True, stop=True), reads=b_grT + [b_wd], writes=[psb])
            P.op("act", lambda e, ps=ps, E1_=E1_: e.activation(out=E1_, in_=ps[:, :], func=AF.Exp, scale=-1.0,
                                                             bias=negbdec[:, h:h + 1]), reads=[psb, b_consts], writes=b_E1g[j])
            P.op("act", lambda e, sp_=sp_, E1_=E1_: e.activation(out=sp_, in_=E1_, func=AF.Ln, bias=1.0),
                 reads=b_E1g[j], writes=b_spg[j])
            P.op("dve", lambda e, sp_=sp_, E2_=E2_: e.tensor_tensor_scan(out=E2_, data0=rmask, data1=sp_, initial=0.0,
                                                                      op0=ALU.mult, op1=ALU.add),
                 reads=b_spg[j] + b_rmask, writes=b_E2g[j])
            P.op("act", lambda e, E1_=E1_, E2_=E2_: e.activation(out=E1_, in_=E2_, func=AF.Exp, scale=-1.0 / 16.0),
                 reads=b_E2g[j], writes=b_E1g[j])
            P.op("pool", lambda e, E1_=E1_, tg=tg: e.tensor_copy(
                out=g.eb[:, tg * 4:(tg + 1) * 4], in_=E1_.rearrange("p (n c) -> p n c", c=128)[:, :, 127]),
                reads=b_E1g[j], writes=[g.b_eb])
            P.op("act", lambda e, E2_=E2_: e.activation(out=E2_, in_=E2_, func=AF.Exp, scale=1.0 / 16.0),
                 reads=b_E2g[j], writes=b_E2g[j])
            if h == 0 and tg == 0:
                dump("E1", E1_, b_E1g[j])
            yield
            ps, psb = nextbank()
            yield from proj_unit(ps, psb, 128, wt, wb, offs[0], tg)
            P.op("dve", lambda e, ps=ps, ts=ts, E1_=E1_: e.scalar_tensor_tensor(out=g.qeT[:, ts], in0=ps[:, :], scalar=128.0 ** -0.5,
                                                                             in1=E1_, op0=ALU.mult, op1=ALU.mult),
                 reads=[psb] + b_E1g[j], writes=g.b_qeT)
            yield
            yield from gate_unit(0, tg)
            ps, psb = nextbank()
            yield from proj_unit(ps, psb, 128, wt, wb, offs[1], tg)
            P.op("dve", lambda e, ps=ps, ts=ts, E2_=E2_: e.tensor_tensor(out=g.keT[:, ts], in0=ps[:, :], in1=E2_, op=ALU.mult),
                 reads=[psb] + b_E2g[j], writes=g.b_keT)
            yield
            yield from gate_unit(1, tg)
        for n4 in range(4):
            ps, psb = nextbank()
            psv = ps[:, 0:256].bitcast(BF16)

            def trk(e, psv=psv, n4=n4):
                last = None
                for q in range(4):
                    n = n4 * 4 + q
                    last = e.transpose(psv[:, q * 128:(q + 1) * 128], g.keT[:, n * 128:(n + 1) * 128], ident16[:])
                return last
            P.op("pe", trk, reads=g.b_keT + [b_consts], writes=[psb])
            P.op("dve", lambda e, psv=psv, n4=n4: e.tensor_copy(
                out=g.ke_tm[:, n4 * 4:(n4 + 1) * 4, :], in_=psv.rearrange("p (q d) -> p q d", q=4)),
                reads=[psb], writes=g.b_ketm)
            yield
        for n2 in range(8):
            ps, psb = nextbank()

            def mmv(e, ps=ps, n2=n2):
                last = None
                for q in range(2):
                    n = n2 * 2 + q
                    for k in range(8):
                        last = e.matmul(ps[:, q * 256:(q + 1) * 256], hT[:, k, n * 128:(n + 1) * 128], wtv[:, k, 0:256],
                                        start=(k == 0), stop=(k == 7))
                return last
            P.op("pe", mmv, reads=[wbv] + b_hTk, writes=[psb])
            P.op("dve", lambda e, ps=ps, n2=n2: e.tensor_copy(
                out=g.v_tm[:, n2 * 2:(n2 + 1) * 2, :], in_=ps[:, :].rearrange("p (q d) -> p q d", q=2)),
                reads=[psb], writes=g.b_vtm)
            yield
    GLA_PRO_N = 20 + 4 + 8 + 16

    def gla_main(h):
        g = gsets[(h + 1) % 2]
        qeT, keT, ke_tm, v_tm = g.qeT, g.keT, g.ke_tm, g.v_tm
        gg16, b_gg = gg16s[(h + 1) % 2], b_ggs[(h + 1) % 2]
        P.op("dve", lambda e: e.memset(Tst[0], 0.0), writes=[b_T[0]] + b_Tblk)
        P.op("dve", lambda e: e.memset(Tst[1], 0.0), writes=[b_T[1]])
        P.op("pool", lambda e: e.memset(S16[0], 0.0), writes=[b_S16[0]])
        O_ps, O_b = [pb[6], pb[7]], [pbb[6], pbb[7]]
        b_Aps = [Buf("Aps0"), Buf("Aps1")]
        G_ps2, G_b2 = [pb[4], pb[5]], [pbb[4], pbb[5]]
        def emit_O(p):
            tok0 = p * 128
            at, atb = at16[p % 2], b_at[p % 2]
            for vc in range(2):
                def mmO(e, vc=vc):
                    e.matmul(O_ps[vc][:, 0:128], S16[p % 3][:, vc * 128:(vc + 1) * 128], qeT[:, tok0:tok0 + 128],
                             start=True, stop=False)
                    return e.matmul(O_ps[vc][:, 0:128], v_tm[:, p, vc * 128:(vc + 1) * 128], at, start=False, stop=True)
                P.op("pe", mmO, reads=g.b_vtm + [atb, b_S16[p % 3]] + g.b_qeT, writes=[O_b[vc]])
                if vc == 0:
                    P.op("act", lambda e: e.activation(out=o32[:, 0, tok0:tok0 + 128], in_=O_ps[0][:, 0:128], func=AF.Copy),
                         reads=[O_b[0]], writes=b_o32[0:4])
                else:
                    P.op("dve", lambda e: e.tensor_copy(out=o32[:, 1, tok0:tok0 + 128], in_=O_ps[1][:, 0:128]),
                         reads=[O_b[1]], writes=b_o32[4:8])

        def emit_chain(p):
            A_ps, A_b = pb[2][:, (p % 2) * 128:(p % 2) * 128 + 128], b_Aps[p % 2]
            tok0 = p * 128
            P.op("pe", lambda e: e.matmul(A_ps, keT[:, tok0:tok0 + 128], qeT[:, tok0:tok0 + 128], start=True, stop=True),
                 reads=g.b_keT + g.b_qeT, writes=[A_b])
            at, atb = at16[p % 2], b_at[p % 2]
            P.op("dve", lambda e: e.tensor_tensor(out=at, in0=A_ps, in1=pairmask, op=ALU.mult),
                 reads=[A_b, b_cst], writes=[atb])
            Gp, Gb = G_ps2[p % 2], G_b2[p % 2]
            P.op("pe", lambda e: e.matmul(Gp[:, 0:256], ke_tm[:, p, :], v_tm[:, p, :], start=True, stop=True),
                 reads=g.b_ketm + g.b_vtm, writes=[Gb])
            Tn, Tnb = Tst[p % 2], b_T[p % 2]
            Tp, Tpb = Tst[(p + 1) % 2], b_T[(p + 1) % 2]
            if p == 0:
                P.op("dve", lambda e: e.tensor_copy(out=Tn, in_=Gp[:, 0:256]), reads=[Gb], writes=[Tnb])
            else:
                ebp = g.eb[:, p - 1:p]
                P.op("dve", lambda e: e.scalar_tensor_tensor(out=Tn, in0=Tp, scalar=ebp, in1=Gp[:, 0:256], op0=ALU.mult, op1=ALU.add),
                     reads=[Gb, Tpb, g.b_eb], writes=[Tnb])
            if p < 15:
                ebn = g.eb[:, p:p + 1]
                sl = (p + 1) % 3
                P.op("dve", lambda e: e.tensor_scalar(out=S16[sl], in0=Tn, scalar1=ebn, scalar2=None, op0=ALU.mult),
                     reads=[Tnb, g.b_eb], writes=[b_S16[sl]])

        for p in range(17):
            if p >= 1:
                emit_O(p - 1)
            if p < 16:
                emit_chain(p)
            yield
        if h == 0:
            dump("o32", RA[:, 5120:9216], b_o32)
        for tg in range(4):
            ts = slice(tg * 512, (tg + 1) * 512)
            for vc in range(2):
                P.op("act", lambda e, vc=vc, ts=ts: e.activation(out=r32(sq_t[:, vc, :]), in_=o32[:, vc, ts], func=AF.Square),
                     reads=b_o32, writes=[b_sq])
            ps, psb = pb[6 + tg % 2], pbb[6 + tg % 2]

            def mmss(e, ps=ps):
                e.matmul(ps[:, :], r32(ones_r[:]), r32(sq_t[:, 0, :]), start=True, stop=False)
                return e.matmul(ps[:, :], r32(ones_r[:]), r32(sq_t[:, 1, :]), start=False, stop=True)
            P.op("pe", mmss, reads=[b_sq, b_consts], writes=[psb])
            P.op("act", lambda e, ps=ps: e.activation(out=rstd_t, in_=ps[:, :], func=AF.Ln, scale=1.0 / 256.0, bias=EPS),
                 reads=[psb], writes=b_rstd)
            P.op("act", lambda e: e.activation(out=rstd_t, in_=rstd_t, func=AF.Exp, scale=-0.5), reads=b_rstd, writes=b_rstd)
            for vc in range(2):
                P.op("dve", lambda e, vc=vc, ts=ts: e.scalar_tensor_tensor(out=tmp_t, in0=o32[:, vc, ts], scalar=glag[:, vc:vc + 1],
                                                                         in1=rstd_t, op0=ALU.mult, op1=ALU.mult),
                     reads=b_o32 + b_rstd + [b_gl], writes=b_tmp)
                P.op("dve", lambda e, vc=vc, ts=ts: e.tensor_tensor(out=og[:, 2 * h + vc, ts], in0=tmp_t, in1=gg16[:, vc, ts], op=ALU.mult),
                     reads=b_tmp + b_gg, writes=[b_og[2 * h + vc]])
            yield
    GLA_MAIN_N = 17 + 4

    class SSet:
        pass
    ssets = []
    for si in range(2):
        s_ = SSet()
        o = si * 4096
        s_.qT = RA[:, o:o + 1024].bitcast(BF16)
        s_.kT = RA[:, o + 1024:o + 2048].bitcast(BF16)
        s_.sv_tm = RA[:, o + 2048:o + 3072].bitcast(BF16).rearrange("p (n d) -> p n d", n=16)
        s_.sg16 = RA[:, o + 3072:o + 4096].bitcast(BF16)
        s_.b_qT, s_.b_kT, s_.b_svtm, s_.b_sg = rb(o, o + 1024), rb(o + 1024, o + 2048), rb(o + 2048, o + 3072), rb(o + 3072, o + 4096)
        ssets.append(s_)
    NE = 3
    e_t = [RA[:, 8192 + i * 512:8192 + (i + 1) * 512] for i in range(NE)]
    b_e = [rb(8192 + i * 512, 8192 + (i + 1) * 512) for i in range(NE)]
    sge, b_sge = RA[:, 10240:10752], rb(10240, 10752)
    NA = 3
    at_t = [gsm[:, i * 256:(i + 1) * 256].bitcast(BF16) for i in range(NA)]
    b_att = [Buf("att%d" % i) for i in range(NA)]
    NSP, NR = 3, 4
    spt = [fr_a[:, i * 512:(i + 1) * 512] for i in range(NSP)]
    R_t = [fr_R[:, i * 512:(i + 1) * 512] for i in range(NR)]
    b_spt = [Buf("sp%d" % i) for i in range(NSP)]
    b_R = [Buf("R%d" % i) for i in range(NR)]
    gsm_all = b_S16 + b_at

    def sb_pro(h):
        s_ = ssets[h % 2]
        wt, wb, offs = wload(w_in, [(C_SQ + h * 128, 128), (C_SK + h * 128, 128)])
        wt2, wb2, offs2 = wload(w_in, [(C_SV + h * 128, 128), (C_SG + h * 128, 128)])
        for tg in range(4):
            ts = slice(tg * 512, (tg + 1) * 512)
            ps, psb = nextbank()
            yield from proj_unit(ps, psb, 128, wt, wb, offs[0], tg)
            P.op("dve", lambda e, ps=ps, ts=ts: e.tensor_scalar(out=s_.qT[:, ts], in0=ps[:, :], scalar1=128.0 ** -0.5, scalar2=None,
                                                             op0=ALU.mult), reads=[psb], writes=s_.b_qT)
            yield
        for tg in range(4):
            ts = slice(tg * 512, (tg + 1) * 512)
            ps, psb = nextbank()
            yield from proj_unit(ps, psb, 128, wt, wb, offs[1], tg)
            P.op("dve", lambda e, ps=ps, ts=ts: e.tensor_copy(out=s_.kT[:, ts], in_=ps[:, :]), reads=[psb], writes=s_.b_kT)
            yield
        for n4 in range(4):
            ps, psb = nextbank()

            def mmv(e, ps=ps, n4=n4, q0=0):
                last = None
                for q in range(q0, q0 + 2):
                    n = n4 * 4 + q
                    for k in range(8):
                        last = e.matmul(ps[:, q * 128:(q + 1) * 128], hT[:, k, n * 128:(n + 1) * 128],
                                        wt2[:, k, offs2[0]:offs2[0] + 128], start=(k == 0), stop=(k == 7))
                return last
            P.op("pe", mmv, reads=[wb2] + b_hTk, writes=[psb])
            yield
            P.op("pe", lambda e, mmv=mmv: mmv(e, q0=2), reads=[wb2] + b_hTk, writes=[psb])
            P.op("dve", lambda e, ps=ps, n4=n4: e.tensor_copy(
                out=s_.sv_tm[:, n4 * 4:(n4 + 1) * 4, :], in_=ps[:, :].rearrange("p (q d) -> p q d", q=4)),
                reads=[psb], writes=s_.b_svtm)
            yield
        for tg in range(4):
            ts = slice(tg * 512, (tg + 1) * 512)
            ps, psb = nextbank()
            yield from proj_unit(ps, psb, 128, wt2, wb2, offs2[1], tg)
            P.op("act", lambda e, ps=ps: e.activation(out=sge, in_=ps[:, :], func=AF.Exp, scale=-1.0), reads=[psb], writes=b_sge)
            P.op("act", lambda e: e.activation(out=sge, in_=sge, func=AF.Ln, bias=1.0), reads=b_sge, writes=b_sge)
            P.op("act", lambda e: e.activation(out=sge, in_=sge, func=AF.Exp, scale=-1.0), reads=b_sge, writes=b_sge)
            P.op("dve", lambda e, ps=ps, ts=ts: e.tensor_tensor(out=s_.sg16[:, ts], in0=ps[:, :], in1=sge, op=ALU.mult),
                 reads=[psb] + b_sge, writes=s_.b_sg)
            yield
    SB_PRO_N = 32

    sb_tiles = []
    for qg in range(4):
        for kb in range(4 * qg + 3, -1, -1):
            c0 = max(0, kb - 4 * qg) * 128
            sb_tiles.append((qg, kb, c0, kb == 4 * qg + 3, kb == 0))
    SB_NT = len(sb_tiles)
    NZB = 4

    SB_DC, SB_DV = 2, 4
    SB_G = 8 * SB_NT

    def sbZ(g):
        h, i = divmod(g, SB_NT)
        s_ = ssets[h % 2]
        qg, kb, c0, first, last = sb_tiles[i]
        N = 512 - c0
        zp, zb = pb[2 + g % NZB], pbb[2 + g % NZB]
        q0 = qg * 512 + c0
        diag = kb >= 4 * qg

        def mmz(e):
            if diag:
                e.matmul(zp[:, 0:128], ident16[:], negmask16[:], start=True, stop=False, skip_group_check=True)
            return e.matmul(zp[:, 0:N], s_.kT[:, kb * 128:(kb + 1) * 128], s_.qT[:, q0:q0 + N], start=not diag, stop=True,
                            skip_group_check=True)
        P.op("pe", mmz, reads=s_.b_kT + s_.b_qT + [b_consts], writes=[zb])
        et, eb_ = e_t[g % NE], b_e[g % NE]
        sp_, spb = spt[g % NSP], b_spt[g % NSP]
        P.op("act", lambda e: e.activation(out=et[:, 0:N], in_=zp[:, 0:N], func=AF.Exp), reads=[zb], writes=eb_)
        P.op("act", lambda e: e.activation(out=r32(sp_[:, 0:N]), in_=et[:, 0:N], func=AF.Ln, bias=1.0), reads=eb_, writes=[spb])
        if first:
            P.op("dve", lambda e: e.tensor_copy(out=r32(R_t[g % NR][:, :]), in_=zeros[:]), reads=[b_consts], writes=[b_R[g % NR]])
        if not last:
            Ro, Rn = R_t[g % NR], R_t[(g + 1) % NR]
            if c0 > 0:
                P.op("dve", lambda e: e.tensor_copy(out=r32(Rn[:, 0:c0]), in_=Ro[:, 0:c0]),
                     reads=[b_R[g % NR]], writes=[b_R[(g + 1) % NR]])
            P.op("dve", lambda e: e.tensor_tensor(out=r32(Rn[:, c0:512]), in0=Ro[:, c0:512], in1=sp_[:, 0:N], op=ALU.add),
                 reads=[b_R[g % NR], spb], writes=[b_R[(g + 1) % NR]])

    def sbC(g):
        h, i = divmod(g, SB_NT)
        qg, kb, c0, first, last = sb_tiles[i]
        N = 512 - c0
        zp, zb = pb[2 + g % NZB], pbb[2 + g % NZB]
        sp_, spb = spt[g % NSP], b_spt[g % NSP]
        Rr, Rb = R_t[g % NR], b_R[g % NR]

        def mmc(e):
            e.matmul(zp[:, 0:N], r32(negU[:]), r32(sp_[:, 0:N]), start=False, stop=False, skip_group_check=True)
            return e.matmul(zp[:, 0:N], r32(negones[:]), r32(Rr[:, c0:512]), start=False, stop=True, skip_group_check=True)
        P.op("pe", mmc, reads=[b_consts, spb, Rb, zb], writes=[zb])
        a_, ab_ = at_t[g % NA], b_att[g % NA]
        P.op("act", lambda e: e.activation(out=a_[:, 0:N], in_=zp[:, 0:N], func=AF.Exp), reads=[zb],
             writes=[ab_] + (gsm_all if g < 3 else []))

    def sbV(g):
        h, i = divmod(g, SB_NT)
        s_ = ssets[h % 2]
        qg, kb, c0, first, last = sb_tiles[i]
        N = 512 - c0
        op_, ob_ = pb[6 + qg % 2], pbb[6 + qg % 2]
        a_, ab_ = at_t[g % NA], b_att[g % NA]
        P.op("pe", lambda e: e.matmul(op_[:, c0:512], s_.sv_tm[:, kb, :], a_[:, 0:N], start=first, stop=last,
                                      skip_group_check=True),
             reads=s_.b_svtm + [ab_], writes=[ob_])
        if last:
            P.op("dve", lambda e: e.tensor_tensor(out=osb[:, h, qg * 512:(qg + 1) * 512], in0=op_[:, :],
                                                 in1=s_.sg16[:, qg * 512:(qg + 1) * 512], op=ALU.mult),
                 reads=[ob_] + s_.b_sg, writes=b_osbk[h])

    def sb_all():
        side = None
        acc = 0.0
        rate = float(SB_PRO_N) / (SB_NT - SB_DV - 4)
        for g in range(SB_G + SB_DV):
            if g < SB_G:
                sbZ(g)
            if 0 <= g - SB_DC < SB_G:
                sbC(g - SB_DC)
            if 0 <= g - SB_DV < SB_G:
                sbV(g - SB_DV)
            h, i = divmod(g, SB_NT)
            if h < 7 and i == SB_DV:
                side = sb_pro(h + 1)
                acc = 0.0
            if side is not None:
                acc += rate
                if i == SB_NT - 1:
                    acc = 1e9
                while side is not None and acc >= 1.0:
                    acc -= 1.0
                    try:
                        next(side)
                    except StopIteration:
                        side = None
    SB_MAIN_N = SB_NT - 4

    def run_all(gen):
        for _ in gen:
            pass

    def interleave(main, side, n_main, n_side):
        acc = 0.0
        for _ in main:
            if side is None:
                continue
            acc += float(n_side) / n_main
            while acc >= 1.0:
                acc -= 1.0
                try:
                    next(side)
                except StopIteration:
                    side = None
                    break
        if side is not None:
            run_all(side)

    STOP_G = stop_after == "gla"
    run_all(gla_pro(0))
    for h in range(4):
        if h < 3:
            interleave(gla_main(h), gla_pro(h + 1), GLA_MAIN_N, GLA_PRO_N)
        elif STOP_G:
            run_all(gla_main(h))
        else:
            interleave(gla_main(h), sb_pro(0), GLA_MAIN_N, SB_PRO_N)
    dump_chunks("og", og, RA[:, 0:2048])
    if STOP_G:
        return finish()
    side_banks[:] = [0, 1]
    sb_all()
    merge_pre = {}
    for c in range(2):
        merge_pre[c] = (wload(w_in, [(C_ML + c * 128, 128), (C_ML + D + c * 128, 128)]),
                        wload(w_pa, [(w_pa, c * 128, 128), (w_pb, c * 128, 128)]))
    P.barrier()
    dump_chunks("osb", osb, RA[:, 11264:11264 + 2048])
    if stop_after == "sb":
        return finish()
    b_osb = [None] * 8

    mg = RA[:, 0:8192].bitcast(BF16).rearrange("p (k t) -> p k t", k=8)
    ga_t = [RA[:, 8192 + i * 512:8192 + (i + 1) * 512] for i in range(2)]
    gb_t = [RA[:, 9216 + i * 512:9216 + (i + 1) * 512] for i in range(2)]
    ya_t = [RA[:, 10240 + i * 512:10240 + (i + 1) * 512] for i in range(2)]
    b_mg = [Buf("mg%d" % k) for k in range(8)]
    b_ga = [Buf("ga0"), Buf("ga1")]
    b_gb = [Buf("gb0"), Buf("gb1")]
    b_ya = [Buf("ya0"), Buf("ya1")]

    def mm8(e, ps, wt, off, src, ts):
        last = None
        for k in range(8):
            last = e.matmul(ps[:, :], wt[:, k, off:off + 128], src[:, k, ts], start=(k == 0), stop=(k == 7))
        return last

    osb_all = [b for k in range(8) for b in b_osbk[k]]

    def merge_pe(c, tg, par, wg, wgb, og_, wy, wyb, oy):
        ts = slice(tg * 512, (tg + 1) * 512)
        B = par * 4
        P.op("pe", lambda e: mm8(e, pb[B + 0], wg, og_[0], hT, ts), reads=[wgb] + b_hTk, writes=[pbb[B + 0]])
        P.op("pe", lambda e: mm8(e, pb[B + 1], wg, og_[1], hT, ts), reads=[wgb] + b_hTk, writes=[pbb[B + 1]])
        P.op("pe", lambda e: mm8(e, pb[B + 2], wy, oy[0], og, ts), reads=[wyb] + b_og, writes=[pbb[B + 2]])
        P.op("pe", lambda e: mm8(e, pb[B + 3], wy, oy[1], osb, ts), reads=[wyb] + osb_all, writes=[pbb[B + 3]])

    def merge_ev(c, tg, par):
        ts = slice(tg * 512, (tg + 1) * 512)
        B = par * 4
        j = par
        P.op("act", lambda e: e.activation(out=ga_t[j], in_=pb[B + 0][:, :], func=AF.Sigmoid, bias=bgate[:, c:c + 1]),
             reads=[pbb[B + 0], b_bg], writes=[b_ga[j]])
        P.op("act", lambda e: e.activation(out=gb_t[j], in_=pb[B + 1][:, :], func=AF.Sigmoid, bias=bgate[:, 8 + c:9 + c]),
             reads=[pbb[B + 1], b_bg], writes=[b_gb[j]])
        P.op("dve", lambda e: e.tensor_tensor(out=ya_t[j], in0=pb[B + 2][:, :], in1=ga_t[j], op=ALU.mult),
             reads=[pbb[B + 2], b_ga[j]], writes=[b_ya[j]])
        P.op("dve", lambda e: e.tensor_tensor(out=gb_t[j], in0=pb[B + 3][:, :], in1=gb_t[j], op=ALU.mult),
             reads=[pbb[B + 3], b_gb[j]], writes=[b_gb[j]])
        P.op("dve", lambda e: e.tensor_tensor(out=mg[:, c, ts], in0=ya_t[j], in1=gb_t[j], op=ALU.add),
             reads=[b_ya[j], b_gb[j]], writes=[b_mg[c]])

    msteps = []
    for c in range(8):
        if c in merge_pre:
            (wg, wgb, og_), (wy, wyb, oy) = merge_pre[c]
        else:
            wg, wgb, og_ = wload(w_in, [(C_ML + c * 128, 128), (C_ML + D + c * 128, 128)])
            wy, wyb, oy = wload(w_pa, [(w_pa, c * 128, 128), (w_pb, c * 128, 128)])
        for tg in range(4):
            msteps.append((c, tg, wg, wgb, og_, wy, wyb, oy))
        while len(msteps) > 0 and (len(msteps) >= 4 or c == 7):
            c_, tg_, a1, a2, a3, a4, a5, a6 = msteps.pop(0)
            sidx = c_ * 4 + tg_
            merge_pe(c_, tg_, sidx % 2, a1, a2, a3, a4, a5, a6)
            if sidx >= 1:
                merge_ev((sidx - 1) // 4, (sidx - 1) % 4, (sidx - 1) % 2)
    merge_ev(7, 3, 31 % 2)
    wo_sl = []
    for i in range(4):
        wo_sl.append(wload(w_o, [(i * 256, 256)]))
    xr = [RA[:, 11264 + i * 1024:11264 + (i + 1) * 1024] for i in range(4)]
    b_xr = [Buf("xr%d" % i) for i in range(4)]
    fing = gg1[:, 0:1024]
    junk2 = gg1[:, 1024:2048]
    b_fing = Buf("fing")
    if not (debug and "mg" in debug):
        P.dma("sp", lambda e: e.dma_start(out=fing, in_=final_g_d.partition_broadcast(128)), writes=[b_fing], sembuf=b_fing)
        for tt in range(4):
            P.dma("sp", lambda e, tt=tt: e.dma_start(out=xr[tt], in_=x_d[tt * 128:(tt + 1) * 128, :]), writes=[b_xr[tt]], sembuf=b_xr[tt])
    P.barrier()
    dump_chunks("mg", mg, RA[:, 12288:12288 + 2048])
    if stop_after == "merge":
        return finish()

    HF = hT[:].rearrange("p k t -> p (k t)").bitcast(F32)
    yo = [HF[:, 4096 + i * 1024:4096 + (i + 1) * 1024] for i in range(4)]
    b_yo = [Buf("yo%d" % i) for i in range(4)]
    b_sm2 = [Buf("sm2_%d" % i) for i in range(4)]
    b_junk2 = Buf("junk2")

    def f_dma(tt):
        j = tt % 4
        P.dma("sp", lambda e: e.dma_start(out=xr[j], in_=x_d[tt * 128:(tt + 1) * 128, :]), writes=[b_xr[j]], sembuf=b_xr[j])

    def f_A(tt):
        j = tt % 4
        xi, xb = xr[j], b_xr[j]
        sm = small[:, 16 + j * 4:16 + j * 4 + 4]
        smb = b_sm2[j]
        for half in range(2):
            ps, psb = pb[(tt * 2 + half) % 8], pbb[(tt * 2 + half) % 8]

            def mmo(e, ps=ps, half=half):
                last = None
                for q in range(2):
                    wt = wo_sl[half * 2 + q][0]
                    for k in range(8):
                        last = e.matmul(ps[:, q * 256:(q + 1) * 256], mg[:, k, tt * 128:(tt + 1) * 128], wt[:, k, 0:256],
                                        start=(k == 0), stop=(k == 7))
                return last
            P.op("pe", mmo, reads=b_mg + [wo_sl[half * 2][1], wo_sl[half * 2 + 1][1]], writes=[psb])
            P.op("dve", lambda e, ps=ps, half=half: e.tensor_tensor(out=xi[:, half * 512:(half + 1) * 512], in0=ps[:, :],
                                                                 in1=xi[:, half * 512:(half + 1) * 512], op=ALU.add),
                 reads=[psb], writes=[xb])
        P.op("act", lambda e: e.activation(out=junk2, in_=xi, func=AF.Square, accum_out=sm[:, 0:1]), reads=[xb], writes=[smb])
        P.op("act", lambda e: e.activation(out=sm[:, 1:2], in_=sm[:, 0:1], func=AF.Ln, scale=1.0 / D, bias=EPS),
             reads=[smb], writes=[smb])
        P.op("act", lambda e: e.activation(out=sm[:, 2:3], in_=sm[:, 1:2], func=AF.Exp, scale=-0.5), reads=[smb], writes=[smb])

    def f_B(tt):
        j = tt % 4
        xi, xb = xr[j], b_xr[j]
        sm = small[:, 16 + j * 4:16 + j * 4 + 4]
        smb = b_sm2[j]
        yi, yb = yo[j], b_yo[j]
        P.op("dve", lambda e: e.scalar_tensor_tensor(out=yi, in0=xi, scalar=sm[:, 2:3], in1=fing, op0=ALU.mult, op1=ALU.mult),
             reads=[xb, smb, b_fing], writes=[yb])
        out_toks.append(P.dma("sp", lambda e: e.dma_start(out=out_d[tt * 128:(tt + 1) * 128, :], in_=yi),
                              reads=[yb], sembuf=yb))

    for step in range(NT + 1):
        if step < NT:
            f_A(step)
        if step >= 1:
            f_B(step - 1)
            if step - 1 + 4 < NT:
                f_dma(step - 1 + 4)
    return finish()


_NC_CACHE = {}


def kernel(x, norm_g, w_in, w_dec_up, b_dec, gla_norm_g, w_pa, w_pb, b_gate, w_o, final_g):
    n = 8
    if "nc" not in _NC_CACHE:
        _NC_CACHE["nc"] = build_nc()
    nc = _NC_CACHE["nc"]
    cst = host_consts()
    f = lambda a: np.ascontiguousarray(np.asarray(a, dtype=np.float32))
    shared = {"norm_g": f(norm_g), "w_in": f(w_in), "w_dec_up": f(w_dec_up), "b_dec": f(b_dec),
              "gla_norm_g": f(gla_norm_g), "w_pa": f(w_pa), "w_pb": f(w_pb), "b_gate": f(b_gate),
              "w_o": f(w_o), "final_g": f(final_g), "cst": cst}
    xs = f(x)
    in_maps = [dict(shared, x=xs[b]) for b in range(n)]
    res = run_bass_kernel_spmd(nc, in_maps, core_ids=list(range(n)))
    return np.stack([r["out"] for r in res.results], axis=0)
```

```python
import os
import numpy as np
import concourse.bass as bass
import concourse.mybir as mybir
from concourse.bass_utils import run_bass_kernel_spmd

F32 = mybir.dt.float32
F32R = mybir.dt.float32r
BF16 = mybir.dt.bfloat16
AF = mybir.ActivationFunctionType
ALU = mybir.AluOpType

D = 1024
T = 2048
NT = 16
C_GQ, C_GK, C_GV, C_GG, C_GR = 0, 512, 1024, 2048, 3072
C_SQ, C_SK, C_SV, C_SG, C_ML = 3088, 4112, 5136, 6160, 7184
NCOL = 9232
EPS = 1e-6
ENGS = ("pe", "act", "dve", "pool", "sp")
SEM_ROT = 2000


class Buf:
    __slots__ = ("name", "w", "r", "dsem")

    def __init__(self, name):
        self.name = name
        self.w = None
        self.r = {}
        self.dsem = None


class Prog:
    def __init__(self, nc):
        self.nc = nc
        self.q = {e: [] for e in ENGS}
        self.cnt = {}
        self.sem = {}
        self.cur = {}
        self.waited = {e: {} for e in ENGS}
        self._ctx = []
        self.nsem = 0
        for e in ENGS:
            self.cur[e] = self._new_sem(e)

    def _new_sem(self, tag):
        key = "%s_%d" % (tag, self.nsem)
        self.nsem += 1
        cm = self.nc.semaphore("s_" + key)
        s = cm.__enter__()
        self._ctx.append(cm)
        self.sem[key] = s
        self.cnt[key] = 0
        return key

    def _mkwaits(self, eng, deps):
        waits = []
        for d in deps:
            if d is None:
                continue
            src, val = d
            if self.waited[eng].get(src, 0) >= val:
                continue
            self.waited[eng][src] = val
            waits.append((src, val))
        return waits

    def _deps(self, reads, writes, extra):
        deps = list(extra)
        for b in reads:
            deps.append(b.w)
        for b in writes:
            deps.append(b.w)
            deps.extend(b.r.values())
        return deps

    def _mark(self, tok, reads, writes):
        for b in reads:
            b.r[tok[0]] = tok
        for b in writes:
            b.w = tok
            b.r = {}

    def op(self, eng, fn, reads=(), writes=(), deps=(), sig=True):
        waits = self._mkwaits(eng, self._deps(reads, writes, deps))
        if not sig:
            self.q[eng].append((waits, fn, None, 0))
            return None
        key = self.cur[eng]
        if self.cnt[key] >= SEM_ROT:
            key = self.cur[eng] = self._new_sem(eng)
        self.cnt[key] += 1
        tok = (key, self.cnt[key])
        self.q[eng].append((waits, fn, key, 1))
        self._mark(tok, reads, writes)
        return tok

    def dma(self, eng, fn, reads=(), writes=(), sembuf=None, deps=()):
        waits = self._mkwaits(eng, self._deps(reads, writes, deps))
        b = sembuf
        if b.dsem is None:
            b.dsem = self._new_sem("d")
        key = b.dsem
        self.cnt[key] += 16
        tok = (key, self.cnt[key])
        self.q[eng].append((waits, fn, key, 16))
        self._mark(tok, reads, writes)
        return tok

    def wait(self, eng, deps):
        waits = self._mkwaits(eng, deps)
        if waits:
            self.q[eng].append((waits, None, None, 0))

    def barrier(self):
        toks = []
        for e in ENGS:
            key = self.cur[e]
            if self.cnt[key] > 0:
                toks.append((key, self.cnt[key]))
        for e in ENGS:
            self.wait(e, toks)

    def emit(self):
        nc = self.nc
        P = self
        with nc.Block() as block:
            def run(engname):
                def body(engine):
                    for waits, fn, sk, inc in P.q[engname]:
                        for src, val in waits:
                            engine.wait_ge(P.sem[src], val)
                        if fn is None:
                            continue
                        inst = fn(engine)
                        if sk is not None:
                            inst.then_inc(P.sem[sk], inc)
                return body
            block.tensor(run("pe"))
            block.scalar(run("act"))
            block.vector(run("dve"))
            block.gpsimd(run("pool"))
            block.sync(run("sp"))

    def close(self):
        for cm in reversed(self._ctx):
            cm.__exit__(None, None, None)
        self._ctx = []


class Alloc:
    def __init__(self, nc):
        self.nc = nc
        self._ctx = []

    def sb(self, name, shape, dt):
        cm = self.nc.sbuf_tensor(name, list(shape), dt)
        t = cm.__enter__()
        self._ctx.append(cm)
        return t

    def ps(self, name, shape, dt):
        cm = self.nc.psum_tensor(name, list(shape), dt)
        t = cm.__enter__()
        self._ctx.append(cm)
        return t

    def close(self):
        for cm in reversed(self._ctx):
            cm.__exit__(None, None, None)
        self._ctx = []


def host_consts():
    c = np.zeros((128, 4, 128), np.float32)
    idx = np.arange(128)
    c[:, 0, :] = np.eye(128, dtype=np.float32)
    j = idx[:, None]
    i = idx[None, :]
    c[:, 1, :] = (i >= j).astype(np.float32)
    c[:, 2, :] = -(j >= i).astype(np.float32)
    c[:, 3, :] = np.where(j >= i, -30000.0, 0.0).astype(np.float32)
    return c


def build_nc(debug=None, stop_after=None):
    nc = bass.Bass("TRN2", target_bir_lowering=False)

    def din(name, shape):
        return nc.dram_tensor(name, list(shape), F32, kind="ExternalInput").ap()

    x_d = din("x", [T, D])
    w_in = din("w_in", [D, NCOL])
    norm_g_d = din("norm_g", [D])
    w_dec_up_d = din("w_dec_up", [16, 512])
    b_dec_d = din("b_dec", [512])
    gla_g_d = din("gla_norm_g", [256])
    w_pa = din("w_pa", [D, D])
    w_pb = din("w_pb", [D, D])
    b_gate_d = din("b_gate", [2 * D])
    w_o = din("w_o", [D, D])
    final_g_d = din("final_g", [D])
    cst_d = din("cst", [128, 4, 128])
    out_d = nc.dram_tensor("out", [T, D], F32, kind="ExternalOutput").ap()
    dbg_d = {}
    if debug:
        for name, shape in debug.items():
            dbg_d[name] = nc.dram_tensor("dbg_" + name, list(shape), F32, kind="ExternalOutput").ap()

    A = Alloc(nc)
    P = Prog(nc)
    out_toks = []

    hT = A.sb("hT", [128, 8, T], BF16)
    og = A.sb("og", [128, 8, T], BF16)
    osb_raw = A.sb("osb", [128, 8192], F32)
    RA = A.sb("RA", [128, 15360], F32)
    fr_a = A.sb("fr_a", [128, 1536], F32)
    fr_R = A.sb("fr_R", [128, 2048], F32)
    zeros = A.sb("zeros", [128, 512], F32)
    NW = 4
    wsl = [A.sb("w%d" % i, [128, 8, 256], BF16) for i in range(NW)]
    wslb = [Buf("w%d" % i) for i in range(NW)]
    cst = A.sb("cst_sb", [128, 4, 128], F32)
    ident16 = A.sb("ident16", [128, 128], BF16)
    negmask16 = A.sb("negmask16", [128, 128], BF16)
    negU = A.sb("negU", [128, 128], F32)
    negones = A.sb("negones", [128, 128], F32)
    ones_r = A.sb("ones_r", [128, 128], F32)
    gvec = A.sb("gvec", [128, 8], F32)
    bgate = A.sb("bgate", [128, 16], F32)
    negbdec = A.sb("negbdec", [128, 4], F32)
    glag = A.sb("glag", [128, 2], F32)
    wdec = A.sb("wdec", [16, 512], F32)
    small = A.sb("small", [128, 128], F32)
    gtmp = A.sb("gtmp", [128, 512], F32)
    gg1 = A.sb("gg1", [128, 2048], F32)
    gsm = A.sb("gsm", [128, 768], F32)
    pb = [A.ps("pb%d" % i, [128, 512], F32) for i in range(8)]
    pbb = [Buf("pb%d" % i) for i in range(8)]

    ident32 = cst[:, 0, :]
    pairmask = cst[:, 1, :]
    osb = osb_raw[:].bitcast(BF16).rearrange("p (k t) -> p k t", k=8)

    b_cst = Buf("cst")
    b_consts = Buf("consts")
    b_hTk = [Buf("hT%d" % k) for k in range(8)]
    b_og = [Buf("og%d" % k) for k in range(8)]
    b_osb = [Buf("osb%d" % k) for k in range(8)]

    def r32(ap):
        return ap.bitcast(F32R)

    b_gv, b_bg, b_nb, b_gl, b_wd = Buf("gv"), Buf("bg"), Buf("nb"), Buf("gl"), Buf("wd")
    P.dma("sp", lambda e: e.dma_start(out=cst[:], in_=cst_d), writes=[b_cst], sembuf=b_cst)
    P.dma("sp", lambda e: e.dma_start(out=gvec[:], in_=norm_g_d.rearrange("(k p) -> p k", p=128),
                                      allow_slow_non_contiguous=True), writes=[b_gv], sembuf=b_gv)
    P.dma("sp", lambda e: e.dma_start(out=bgate[:], in_=b_gate_d.rearrange("(k p) -> p k", p=128),
                                      allow_slow_non_contiguous=True), writes=[b_bg], sembuf=b_bg)
    P.dma("sp", lambda e: e.dma_start(out=negbdec[:], in_=b_dec_d.rearrange("(k p) -> p k", p=128),
                                      allow_slow_non_contiguous=True), writes=[b_nb], sembuf=b_nb)
    P.dma("sp", lambda e: e.dma_start(out=glag[:], in_=gla_g_d.rearrange("(k p) -> p k", p=128),
                                      allow_slow_non_contiguous=True), writes=[b_gl], sembuf=b_gl)
    P.dma("sp", lambda e: e.dma_start(out=wdec[:], in_=w_dec_up_d), writes=[b_wd], sembuf=b_wd)
    P.op("dve", lambda e: e.tensor_copy(out=ident16[:], in_=ident32), reads=[b_cst], writes=[b_consts])
    P.op("dve", lambda e: e.tensor_copy(out=negmask16[:], in_=cst[:, 3, :]), reads=[b_cst], writes=[b_consts])
    P.op("dve", lambda e: e.tensor_copy(out=r32(negU[:]), in_=cst[:, 2, :]), reads=[b_cst], writes=[b_consts])
    P.op("dve", lambda e: e.memset(RA[:, 14848:14976], -1.0), writes=[b_consts])
    P.op("dve", lambda e: e.tensor_copy(out=r32(negones[:]), in_=RA[:, 14848:14976]), writes=[b_consts])
    P.op("dve", lambda e: e.memset(RA[:, 15104:15232], 1.0), writes=[b_consts])
    P.op("dve", lambda e: e.tensor_copy(out=r32(ones_r[:]), in_=RA[:, 15104:15232]), writes=[b_consts])
    P.op("dve", lambda e: e.memset(zeros[:], 0.0), writes=[b_consts])
    P.op("dve", lambda e: e.tensor_scalar(out=negbdec[:], in0=negbdec[:], scalar1=-1.0, scalar2=None,
                                          op0=ALU.mult), reads=[b_nb], writes=[b_consts])

    if stop_after == "consts":
        P.wait("sp", out_toks)
        P.emit(); P.close(); A.close()
        return nc

    wstate = {"i": 0}

    def wload(src, pieces):
        i = wstate["i"] % NW
        wstate["i"] += 1
        t, b = wsl[i], wslb[i]
        offs = []
        o = 0
        for pc in pieces:
            if len(pc) == 3:
                srcp, c0, n = pc
            else:
                srcp = src
                c0, n = pc
            sv = srcp.rearrange("(k p) c -> p k c", p=128)
            P.dma("pool", lambda e, o=o, c0=c0, n=n, t=t, sv=sv: e.dma_start(out=t[:, :, o:o + n], in_=sv[:, :, c0:c0 + n]),
                  writes=[b], sembuf=b)
            offs.append(o)
            o += n
        assert o <= 256
        return t, b, offs

    pbrot = {"i": 0}

    def proj_fm(wt, wb, off, m, rhs_fn, rhs_bufs, evac, banks=(0, 1)):
        for tg in range(4):
            bi = banks[pbrot["i"] % len(banks)]
            pbrot["i"] += 1
            ps, psb = pb[bi], pbb[bi]

            def mm(e, ps=ps, tg=tg):
                last = None
                for k in range(8):
                    last = e.matmul(ps[0:m, :], wt[:, k, off:off + m], rhs_fn(k, tg),
                                    start=(k == 0), stop=(k == 7))
                return last
            P.op("pe", mm, reads=[wb] + list(rhs_bufs), writes=[psb])
            evac(tg, ps, psb)

    def hT_rhs(k, tg):
        return hT[:, k, tg * 512:(tg + 1) * 512]

    NXB = 4
    xt = [RA[:, i * 1024:(i + 1) * 1024] for i in range(NXB)]
    xtb = [Buf("xt%d" % i) for i in range(NXB)]
    junk = RA[:, 4096:5120]
    b_junk = Buf("junk")
    b_small = [Buf("small%d" % i) for i in range(4)]
    def p1_dma(tt):
        xi, xb = xt[tt % NXB], xtb[tt % NXB]
        P.dma("sp", lambda e: e.dma_start(out=xi, in_=x_d[tt * 128:(tt + 1) * 128, :]), writes=[xb], sembuf=xb)

    def p1_A(tt):
        xi, xb = xt[tt % NXB], xtb[tt % NXB]
        sm = small[:, (tt % NXB) * 4:(tt % NXB) * 4 + 4]
        smb = b_small[tt % NXB]
        P.op("act", lambda e: e.activation(out=junk, in_=xi, func=AF.Square, accum_out=sm[:, 0:1]), reads=[xb], writes=[smb])
        P.op("act", lambda e: e.activation(out=sm[:, 1:2], in_=sm[:, 0:1], func=AF.Ln, scale=1.0 / D, bias=EPS),
             reads=[smb], writes=[smb])
        P.op("act", lambda e: e.activation(out=sm[:, 2:3], in_=sm[:, 1:2], func=AF.Exp, scale=-0.5), reads=[smb], writes=[smb])
        P.op("dve", lambda e: e.tensor_scalar(out=xi, in0=xi, scalar1=sm[:, 2:3], scalar2=None, op0=ALU.mult),
             reads=[smb], writes=[xb])

    def p1_B(tt):
        xi, xb = xt[tt % NXB], xtb[tt % NXB]
        for half in range(2):
            bi = (tt * 2 + half) % 4
            ps, psb = pb[bi], pbb[bi]

            def tr(e, ps=ps, half=half):
                last = None
                for q in range(4):
                    k = half * 4 + q
                    last = e.transpose(ps[:, q * 128:(q + 1) * 128], xi[:, k * 128:(k + 1) * 128], ident32)
                return last
            P.op("pe", tr, reads=[xb, b_cst], writes=[psb])
            for q in range(4):
                k = half * 4 + q
                if half == 0:
                    P.op("dve", lambda e, ps=ps, q=q, k=k: e.tensor_scalar(
                        out=hT[:, k, tt * 128:(tt + 1) * 128], in0=ps[:, q * 128:(q + 1) * 128],
                        scalar1=gvec[:, k:k + 1], scalar2=None, op0=ALU.mult), reads=[psb, b_gv], writes=[b_hTk[k]])
                else:
                    P.op("act", lambda e, ps=ps, q=q, k=k: e.activation(
                        out=hT[:, k, tt * 128:(tt + 1) * 128], in_=ps[:, q * 128:(q + 1) * 128],
                        func=AF.Copy, scale=gvec[:, k:k + 1]), reads=[psb, b_gv], writes=[b_hTk[k]])

    for tt in range(NXB):
        p1_dma(tt)
    for step in range(NT + 1):
        if step < NT:
            p1_A(step)
        if step >= 1:
            p1_B(step - 1)
            if step - 1 + NXB < NT:
                p1_dma(step - 1 + NXB)
    P.barrier()

    def dump(name, ap, bufs=()):
        if debug and name in debug:
            out_toks.append(P.dma("sp", lambda e: e.dma_start(out=dbg_d[name], in_=ap), reads=list(bufs), sembuf=Buf("dbg")))

    b_dtmp = Buf("dtmp")

    def dump_chunks(name, src3d, tmp):
        if not (debug and name in debug):
            return
        P.barrier()
        for k in range(8):
            P.op("dve", lambda e, k=k: e.tensor_copy(out=tmp, in_=src3d[:, k, :]), writes=[b_dtmp])
            out_toks.append(P.dma("sp", lambda e, k=k: e.dma_start(out=dbg_d[name][k], in_=tmp), reads=[b_dtmp], sembuf=b_dtmp))
        P.barrier()

    def finish():
        P.barrier()
        P.wait("sp", out_toks)
        P.emit()
        P.close()
        A.close()
        return nc

    dump_chunks("hT", hT, RA[:, 4096:4096 + 2048])
    if stop_after == "hT":
        return finish()

    RB = [Buf("RA%d" % i) for i in range(30)]
    OB = [Buf("OS%d" % i) for i in range(16)]

    def rb(a, b):
        return RB[a // 512:(b + 511) // 512]

    def ob(a, b):
        return OB[a // 512:(b + 511) // 512]

    for k in range(8):
        b_osb[k] = None
    b_osbk = [ob(k * 1024, (k + 1) * 1024) for k in range(8)]

    side_banks = [0, 1, 3]

    def nextbank():
        bi = side_banks[pbrot["i"] % len(side_banks)]
        pbrot["i"] += 1
        return pb[bi], pbb[bi]

    def mm8h(e, ps, m, wt, off, tg, k0=0, k1=8):
        last = None
        for k in range(k0, k1):
            last = e.matmul(ps[0:m, :], wt[:, k, off:off + m], hT[:, k, tg * 512:(tg + 1) * 512], start=(k == 0), stop=(k == 7))
        return last

    def proj_unit(ps, psb, m, wt, wb, off, tg):
        P.op("pe", lambda e: mm8h(e, ps, m, wt, off, tg, 0, 4), reads=[wb] + b_hTk, writes=[psb], sig=False)
        yield
        P.op("pe", lambda e: mm8h(e, ps, m, wt, off, tg, 4, 8), reads=[wb] + b_hTk, writes=[psb])

    class GSet:
        pass
    gsets = []
    for si in range(2):
        g = GSet()
        base, bf = (osb_raw, ob) if si == 0 else (RA, rb)
        g.qeT = base[:, 0:1024].bitcast(BF16)
        g.keT = base[:, 1024:2048].bitcast(BF16)
        g.ke_tm = base[:, 2048:3072].bitcast(BF16).rearrange("p (n d) -> p n d", n=16)
        g.v_tm = base[:, 3072:5120].bitcast(BF16).rearrange("p (n d) -> p n d", n=16)
        g.b_qeT, g.b_keT, g.b_ketm, g.b_vtm = bf(0, 1024), bf(1024, 2048), bf(2048, 3072), bf(3072, 5120)
        g.eb = small[:, 64 + si * 32:64 + (si + 1) * 32]
        g.b_eb = Buf("eb%d" % si)
        gsets.append(g)
    gg16s = [osb_raw[:, 5120:7168].bitcast(BF16).rearrange("p (v t) -> p v t", v=2),
             gg1[:, :].bitcast(BF16).rearrange("p (v t) -> p v t", v=2)]
    b_ggs = [ob(5120, 7168), [Buf("gg1a"), Buf("gg1b")]]
    rstd_t, b_rstd = osb_raw[:, 7168:7680], ob(7168, 7680)
    tmp_t, b_tmp = osb_raw[:, 7680:8192], ob(7680, 8192)
    o32 = RA[:, 5120:9216].rearrange("p (v t) -> p v t", v=2)
    b_o32 = rb(5120, 9216)
    grT, b_grT = RA[:, 9216:11264], rb(9216, 11264)
    sp_g = [RA[:, 11264 + i * 512:11264 + (i + 1) * 512] for i in range(2)]
    E1_g = [RA[:, 12288 + i * 512:12288 + (i + 1) * 512] for i in range(2)]
    E2_g = [RA[:, 13312 + i * 512:13312 + (i + 1) * 512] for i in range(2)]
    b_spg = [rb(11264 + i * 512, 11264 + (i + 1) * 512) for i in range(2)]
    b_E1g = [rb(12288 + i * 512, 12288 + (i + 1) * 512) for i in range(2)]
    b_E2g = [rb(13312 + i * 512, 13312 + (i + 1) * 512) for i in range(2)]
    rmask, b_rmask = RA[:, 14336:14848], rb(14336, 14848)
    Tst = [RA[:, 14848 + i * 256:14848 + (i + 1) * 256] for i in range(2)]
    b_T = [Buf("T0"), Buf("T1")]
    b_Tblk = rb(14848, 15360)
    S16 = [gsm[:, i * 128:(i + 1) * 128].bitcast(BF16) for i in range(3)]
    at16 = [gsm[:, 384 + i * 64:384 + (i + 1) * 64].bitcast(BF16) for i in range(2)]
    b_S16 = [Buf("S16_%d" % i) for i in range(3)]
    b_at = [Buf("at0"), Buf("at1")]
    b_gtmp = Buf("gtmp")
    sq_t = fr_a[:, 0:1024].rearrange("p (v t) -> p v t", v=2)
    b_sq = Buf("sq")

    P.op("pool", lambda e: e.memset(rmask, 1.0), writes=b_rmask)
    P.op("pool", lambda e: e.memset(rmask.rearrange("p (n c) -> p n c", c=128)[:, :, 0:1], 0.0), writes=b_rmask)

    wt_r, wb_r, offs_r = wload(w_in, [(C_GR, 16)])
    for tg in range(4):
        ps, psb = nextbank()
        P.op("pe", lambda e, ps=ps, tg=tg: mm8h(e, ps, 16, wt_r, offs_r[0], tg), reads=[wb_r] + b_hTk, writes=[psb])
        P.op("dve", lambda e, ps=ps, tg=tg: e.tensor_copy(out=grT[0:16, tg * 512:(tg + 1) * 512], in_=ps[0:16, :]),
             reads=[psb], writes=b_grT)

    def gla_pro(h):
        g = gsets[(h + 1) % 2]
        wt, wb, offs = wload(w_in, [(C_GQ + h * 128, 128), (C_GK + h * 128, 128)])
        wtv, wbv, _ = wload(w_in, [(C_GV + h * 256, 256)])
        wtg, wbg, _ = wload(w_in, [(C_GG + h * 256, 256)])
        gg16, b_gg = gg16s[(h + 1) % 2], b_ggs[(h + 1) % 2]
        b_gt = [b_gtmp]

        def gate_unit(vc, tg):
            ts = slice(tg * 512, (tg + 1) * 512)
            ps, psb = nextbank()
            yield from proj_unit(ps, psb, 128, wtg, wbg, vc * 128, tg)
            P.op("act", lambda e: e.activation(out=gtmp[:, :], in_=ps[:, :], func=AF.Exp, scale=-1.0), reads=[psb], writes=b_gt)
            P.op("act", lambda e: e.activation(out=gtmp[:, :], in_=gtmp[:, :], func=AF.Ln, bias=1.0), reads=b_gt, writes=b_gt)
            P.op("act", lambda e: e.activation(out=gtmp[:, :], in_=gtmp[:, :], func=AF.Exp, scale=-1.0), reads=b_gt, writes=b_gt)
            P.op("dve", lambda e: e.tensor_tensor(out=gg16[:, vc, ts], in0=ps[:, :], in1=gtmp[:, :], op=ALU.mult),
                 reads=[psb] + b_gt, writes=b_gg)
            yield

        for tg in range(4):
            j = tg % 2
            ts = slice(tg * 512, (tg + 1) * 512)
            sp_, E1_, E2_ = sp_g[j], E1_g[j], E2_g[j]
            ps, psb = nextbank()
            P.op("pe", lambda e, ps=ps, ts=ts: e.matmul(ps[:, :], wdec[0:16, h * 128:(h + 1) * 128], grT[0:16, ts],
                                                       start=True, stop=True), reads=b_grT + [b_wd], writes=[psb])
            P.op("act", lambda e, ps=ps, E1_=E1_: e.activation(out=E1_, in_=ps[:, :], func=AF.Exp, scale=-1.0,
                                                             bias=negbdec[:, h:h + 1]), reads=[psb, b_consts], writes=b_E1g[j])
            P.op("act", lambda e, sp_=sp_, E1_=E1_: e.activation(out=sp_, in_=E1_, func=AF.Ln, bias=1.0),
                 reads=b_E1g[j], writes=b_spg[j])
            P.op("dve", lambda e, sp_=sp_, E2_=E2_: e.tensor_tensor_scan(out=E2_, data0=rmask, data1=sp_, initial=0.0,
                                                                      op0=ALU.mult, op1=ALU.add),
                 reads=b_spg[j] + b_rmask, writes=b_E2g[j])
            P.op("act", lambda e, E1_=E1_, E2_=E2_: e.activation(out=E1_, in_=E2_, func=AF.Exp, scale=-1.0 / 16.0),
                 reads=b_E2g[j], writes=b_E1g[j])
            P.op("pool", lambda e, E1_=E1_, tg=tg: e.tensor_copy(
                out=g.eb[:, tg * 4:(tg + 1) * 4], in_=E1_.rearrange("p (n c) -> p n c", c=128)[:, :, 127]),
                reads=b_E1g[j], writes=[g.b_eb])
            P.op("act", lambda e, E2_=E2_: e.activation(out=E2_, in_=E2_, func=AF.Exp, scale=1.0 / 16.0),
                 reads=b_E2g[j], writes=b_E2g[j])
            if h == 0 and tg == 0:
                dump("E1", E1_, b_E1g[j])
            yield
            ps, psb = nextbank()
            yield from proj_unit(ps, psb, 128, wt, wb, offs[0], tg)
            P.op("dve", lambda e, ps=ps, ts=ts, E1_=E1_: e.scalar_tensor_tensor(out=g.qeT[:, ts], in0=ps[:, :], scalar=128.0 ** -0.5,
                                                                             in1=E1_, op0=ALU.mult, op1=ALU.mult),
                 reads=[psb] + b_E1g[j], writes=g.b_qeT)
            yield
            yield from gate_unit(0, tg)
            ps, psb = nextbank()
            yield from proj_unit(ps, psb, 128, wt, wb, offs[1], tg)
            P.op("dve", lambda e, ps=ps, ts=ts, E2_=E2_: e.tensor_tensor(out=g.keT[:, ts], in0=ps[:, :], in1=E2_, op=ALU.mult),
                 reads=[psb] + b_E2g[j], writes=g.b_keT)
            yield
            yield from gate_unit(1, tg)
        for n4 in range(4):
            ps, psb = nextbank()
            psv = ps[:, 0:256].bitcast(BF16)

            def trk(e, psv=psv, n4=n4):
                last = None
                for q in range(4):
                    n = n4 * 4 + q
                    last = e.transpose(psv[:, q * 128:(q + 1) * 128], g.keT[:, n * 128:(n + 1) * 128], ident16[:])
                return last
            P.op("pe", trk, reads=g.b_keT + [b_consts], writes=[psb])
            P.op("dve", lambda e, psv=psv, n4=n4: e.tensor_copy(
                out=g.ke_tm[:, n4 * 4:(n4 + 1) * 4, :], in_=psv.rearrange("p (q d) -> p q d", q=4)),
                reads=[psb], writes=g.b_ketm)
            yield
        for n2 in range(8):
            ps, psb = nextbank()

            def mmv(e, ps=ps, n2=n2):
                last = None
                for q in range(2):
                    n = n2 * 2 + q
                    for k in range(8):
                        last = e.matmul(ps[:, q * 256:(q + 1) * 256], hT[:, k, n * 128:(n + 1) * 128], wtv[:, k, 0:256],
                                        start=(k == 0), stop=(k == 7))
                return last
            P.op("pe", mmv, reads=[wbv] + b_hTk, writes=[psb])
            P.op("dve", lambda e, ps=ps, n2=n2: e.tensor_copy(
                out=g.v_tm[:, n2 * 2:(n2 + 1) * 2, :], in_=ps[:, :].rearrange("p (q d) -> p q d", q=2)),
                reads=[psb], writes=g.b_vtm)
            yield
    GLA_PRO_N = 20 + 4 + 8 + 16

    def gla_main(h):
        g = gsets[(h + 1) % 2]
        qeT, keT, ke_tm, v_tm = g.qeT, g.keT, g.ke_tm, g.v_tm
        gg16, b_gg = gg16s[(h + 1) % 2], b_ggs[(h + 1) % 2]
        P.op("dve", lambda e: e.memset(Tst[0], 0.0), writes=[b_T[0]] + b_Tblk)
        P.op("dve", lambda e: e.memset(Tst[1], 0.0), writes=[b_T[1]])
        P.op("pool", lambda e: e.memset(S16[0], 0.0), writes=[b_S16[0]])
        O_ps, O_b = [pb[6], pb[7]], [pbb[6], pbb[7]]
        b_Aps = [Buf("Aps0"), Buf("Aps1")]
        G_ps2, G_b2 = [pb[4], pb[5]], [pbb[4], pbb[5]]
        def emit_O(p):
            tok0 = p * 128
            at, atb = at16[p % 2], b_at[p % 2]
            for vc in range(2):
                def mmO(e, vc=vc):
                    e.matmul(O_ps[vc][:, 0:128], S16[p % 3][:, vc * 128:(vc + 1) * 128], qeT[:, tok0:tok0 + 128],
                             start=True, stop=False)
                    return e.matmul(O_ps[vc][:, 0:128], v_tm[:, p, vc * 128:(vc + 1) * 128], at, start=False, stop=True)
                P.op("pe", mmO, reads=g.b_vtm + [atb, b_S16[p % 3]] + g.b_qeT, writes=[O_b[vc]])
                if vc == 0:
                    P.op("act", lambda e: e.activation(out=o32[:, 0, tok0:tok0 + 128], in_=O_ps[0][:, 0:128], func=AF.Copy),
                         reads=[O_b[0]], writes=b_o32[0:4])
                else:
                    P.op("dve", lambda e: e.tensor_copy(out=o32[:, 1, tok0:tok0 + 128], in_=O_ps[1][:, 0:128]),
                         reads=[O_b[1]], writes=b_o32[4:8])

        def emit_chain(p):
            A_ps, A_b = pb[2][:, (p % 2) * 128:(p % 2) * 128 + 128], b_Aps[p % 2]
            tok0 = p * 128
            P.op("pe", lambda e: e.matmul(A_ps, keT[:, tok0:tok0 + 128], qeT[:, tok0:tok0 + 128], start=True, stop=True),
                 reads=g.b_keT + g.b_qeT, writes=[A_b])
            at, atb = at16[p % 2], b_at[p % 2]
            P.op("dve", lambda e: e.tensor_tensor(out=at, in0=A_ps, in1=pairmask, op=ALU.mult),
                 reads=[A_b, b_cst], writes=[atb])
            Gp, Gb = G_ps2[p % 2], G_b2[p % 2]
            P.op("pe", lambda e: e.matmul(Gp[:, 0:256], ke_tm[:, p, :], v_tm[:, p, :], start=True, stop=True),
                 reads=g.b_ketm + g.b_vtm, writes=[Gb])
            Tn, Tnb = Tst[p % 2], b_T[p % 2]
            Tp, Tpb = Tst[(p + 1) % 2], b_T[(p + 1) % 2]
            if p == 0:
                P.op("dve", lambda e: e.tensor_copy(out=Tn, in_=Gp[:, 0:256]), reads=[Gb], writes=[Tnb])
            else:
                ebp = g.eb[:, p - 1:p]
                P.op("dve", lambda e: e.scalar_tensor_tensor(out=Tn, in0=Tp, scalar=ebp, in1=Gp[:, 0:256], op0=ALU.mult, op1=ALU.add),
                     reads=[Gb, Tpb, g.b_eb], writes=[Tnb])
            if p < 15:
                ebn = g.eb[:, p:p + 1]
                sl = (p + 1) % 3
                P.op("dve", lambda e: e.tensor_scalar(out=S16[sl], in0=Tn, scalar1=ebn, scalar2=None, op0=ALU.mult),
                     reads=[Tnb, g.b_eb], writes=[b_S16[sl]])

        for p in range(17):
            if p >= 1:
                emit_O(p - 1)
            if p < 16:
                emit_chain(p)
            yield
        if h == 0:
            dump("o32", RA[:, 5120:9216], b_o32)
        for tg in range(4):
            ts = slice(tg * 512, (tg + 1) * 512)
            for vc in range(2):
                P.op("act", lambda e, vc=vc, ts=ts: e.activation(out=r32(sq_t[:, vc, :]), in_=o32[:, vc, ts], func=AF.Square),
                     reads=b_o32, writes=[b_sq])
            ps, psb = pb[6 + tg % 2], pbb[6 + tg % 2]

            def mmss(e, ps=ps):
                e.matmul(ps[:, :], r32(ones_r[:]), r32(sq_t[:, 0, :]), start=True, stop=False)
                return e.matmul(ps[:, :], r32(ones_r[:]), r32(sq_t[:, 1, :]), start=False, stop=True)
            P.op("pe", mmss, reads=[b_sq, b_consts], writes=[psb])
            P.op("act", lambda e, ps=ps: e.activation(out=rstd_t, in_=ps[:, :], func=AF.Ln, scale=1.0 / 256.0, bias=EPS),
                 reads=[psb], writes=b_rstd)
            P.op("act", lambda e: e.activation(out=rstd_t, in_=rstd_t, func=AF.Exp, scale=-0.5), reads=b_rstd, writes=b_rstd)
            for vc in range(2):
                P.op("dve", lambda e, vc=vc, ts=ts: e.scalar_tensor_tensor(out=tmp_t, in0=o32[:, vc, ts], scalar=glag[:, vc:vc + 1],
                                                                         in1=rstd_t, op0=ALU.mult, op1=ALU.mult),
                     reads=b_o32 + b_rstd + [b_gl], writes=b_tmp)
                P.op("dve", lambda e, vc=vc, ts=ts: e.tensor_tensor(out=og[:, 2 * h + vc, ts], in0=tmp_t, in1=gg16[:, vc, ts], op=ALU.mult),
                     reads=b_tmp + b_gg, writes=[b_og[2 * h + vc]])
            yield
    GLA_MAIN_N = 17 + 4

    class SSet:
        pass
    ssets = []
    for si in range(2):
        s_ = SSet()
        o = si * 4096
        s_.qT = RA[:, o:o + 1024].bitcast(BF16)
        s_.kT = RA[:, o + 1024:o + 2048].bitcast(BF16)
        s_.sv_tm = RA[:, o + 2048:o + 3072].bitcast(BF16).rearrange("p (n d) -> p n d", n=16)
        s_.sg16 = RA[:, o + 3072:o + 4096].bitcast(BF16)
        s_.b_qT, s_.b_kT, s_.b_svtm, s_.b_sg = rb(o, o + 1024), rb(o + 1024, o + 2048), rb(o + 2048, o + 3072), rb(o + 3072, o + 4096)
        ssets.append(s_)
    NE = 3
    e_t = [RA[:, 8192 + i * 512:8192 + (i + 1) * 512] for i in range(NE)]
    b_e = [rb(8192 + i * 512, 8192 + (i + 1) * 512) for i in range(NE)]
    sge, b_sge = RA[:, 10240:10752], rb(10240, 10752)
    NA = 3
    at_t = [gsm[:, i * 256:(i + 1) * 256].bitcast(BF16) for i in range(NA)]
    b_att = [Buf("att%d" % i) for i in range(NA)]
    NSP, NR = 3, 4
    spt = [fr_a[:, i * 512:(i + 1) * 512] for i in range(NSP)]
    R_t = [fr_R[:, i * 512:(i + 1) * 512] for i in range(NR)]
    b_spt = [Buf("sp%d" % i) for i in range(NSP)]
    b_R = [Buf("R%d" % i) for i in range(NR)]
    gsm_all = b_S16 + b_at

    def sb_pro(h):
        s_ = ssets[h % 2]
        wt, wb, offs = wload(w_in, [(C_SQ + h * 128, 128), (C_SK + h * 128, 128)])
        wt2, wb2, offs2 = wload(w_in, [(C_SV + h * 128, 128), (C_SG + h * 128, 128)])
        for tg in range(4):
            ts = slice(tg * 512, (tg + 1) * 512)
            ps, psb = nextbank()
            yield from proj_unit(ps, psb, 128, wt, wb, offs[0], tg)
            P.op("dve", lambda e, ps=ps, ts=ts: e.tensor_scalar(out=s_.qT[:, ts], in0=ps[:, :], scalar1=128.0 ** -0.5, scalar2=None,
                                                             op0=ALU.mult), reads=[psb], writes=s_.b_qT)
            yield
        for tg in range(4):
            ts = slice(tg * 512, (tg + 1) * 512)
            ps, psb = nextbank()
            yield from proj_unit(ps, psb, 128, wt, wb, offs[1], tg)
            P.op("dve", lambda e, ps=ps, ts=ts: e.tensor_copy(out=s_.kT[:, ts], in_=ps[:, :]), reads=[psb], writes=s_.b_kT)
            yield
        for n4 in range(4):
            ps, psb = nextbank()

            def mmv(e, ps=ps, n4=n4, q0=0):
                last = None
                for q in range(q0, q0 + 2):
                    n = n4 * 4 + q
                    for k in range(8):
                        last = e.matmul(ps[:, q * 128:(q + 1) * 128], hT[:, k, n * 128:(n + 1) * 128],
                                        wt2[:, k, offs2[0]:offs2[0] + 128], start=(k == 0), stop=(k == 7))
                return last
            P.op("pe", mmv, reads=[wb2] + b_hTk, writes=[psb], sig=False)
            yield
            P.op("pe", lambda e, mmv=mmv: mmv(e, q0=2), reads=[wb2] + b_hTk, writes=[psb])
            P.op("dve", lambda e, ps=ps, n4=n4: e.tensor_copy(
                out=s_.sv_tm[:, n4 * 4:(n4 + 1) * 4, :], in_=ps[:, :].rearrange("p (q d) -> p q d", q=4)),
                reads=[psb], writes=s_.b_svtm)
            yield
        for tg in range(4):
            ts = slice(tg * 512, (tg + 1) * 512)
            ps, psb = nextbank()
            yield from proj_unit(ps, psb, 128, wt2, wb2, offs2[1], tg)
            P.op("act", lambda e, ps=ps: e.activation(out=sge, in_=ps[:, :], func=AF.Exp, scale=-1.0), reads=[psb], writes=b_sge)
            P.op("act", lambda e: e.activation(out=sge, in_=sge, func=AF.Ln, bias=1.0), reads=b_sge, writes=b_sge)
            P.op("act", lambda e: e.activation(out=sge, in_=sge, func=AF.Exp, scale=-1.0), reads=b_sge, writes=b_sge)
            P.op("dve", lambda e, ps=ps, ts=ts: e.tensor_tensor(out=s_.sg16[:, ts], in0=ps[:, :], in1=sge, op=ALU.mult),
                 reads=[psb] + b_sge, writes=s_.b_sg)
            yield
    SB_PRO_N = 32

    sb_tiles = []
    for qg in range(4):
        for kb in range(4 * qg + 3, -1, -1):
            c0 = max(0, kb - 4 * qg) * 128
            sb_tiles.append((qg, kb, c0, kb == 4 * qg + 3, kb == 0))
    SB_NT = len(sb_tiles)
    NZB = 4

    SB_DC, SB_DV = 2, 4
    SB_G = 8 * SB_NT

    def sbZ(g):
        h, i = divmod(g, SB_NT)
        s_ = ssets[h % 2]
        qg, kb, c0, first, last = sb_tiles[i]
        N = 512 - c0
        zp, zb = pb[2 + g % NZB], pbb[2 + g % NZB]
        q0 = qg * 512 + c0
        diag = kb >= 4 * qg

        def mmz(e):
            if diag:
                e.matmul(zp[:, 0:128], ident16[:], negmask16[:], start=True, stop=False, skip_group_check=True)
            return e.matmul(zp[:, 0:N], s_.kT[:, kb * 128:(kb + 1) * 128], s_.qT[:, q0:q0 + N], start=not diag, stop=True,
                            skip_group_check=True)
        P.op("pe", mmz, reads=s_.b_kT + s_.b_qT + [b_consts], writes=[zb])
        et, eb_ = e_t[g % NE], b_e[g % NE]
        sp_, spb = spt[g % NSP], b_spt[g % NSP]
        P.op("act", lambda e: e.activation(out=et[:, 0:N], in_=zp[:, 0:N], func=AF.Exp), reads=[zb], writes=eb_)
        P.op("act", lambda e: e.activation(out=r32(sp_[:, 0:N]), in_=et[:, 0:N], func=AF.Ln, bias=1.0), reads=eb_, writes=[spb])
        if first:
            P.op("dve", lambda e: e.tensor_copy(out=r32(R_t[g % NR][:, :]), in_=zeros[:]), reads=[b_consts], writes=[b_R[g % NR]])
        if not last:
            Ro, Rn = R_t[g % NR], R_t[(g + 1) % NR]
            if c0 > 0:
                P.op("dve", lambda e: e.tensor_copy(out=r32(Rn[:, 0:c0]), in_=Ro[:, 0:c0]),
                     reads=[b_R[g % NR]], writes=[b_R[(g + 1) % NR]])
            P.op("dve", lambda e: e.tensor_tensor(out=r32(Rn[:, c0:512]), in0=Ro[:, c0:512], in1=sp_[:, 0:N], op=ALU.add),
                 reads=[b_R[g % NR], spb], writes=[b_R[(g + 1) % NR]])

    def sbC(g):
        h, i = divmod(g, SB_NT)
        qg, kb, c0, first, last = sb_tiles[i]
        N = 512 - c0
        zp, zb = pb[2 + g % NZB], pbb[2 + g % NZB]
        sp_, spb = spt[g % NSP], b_spt[g % NSP]
        Rr, Rb = R_t[g % NR], b_R[g % NR]

        def mmc(e):
            e.matmul(zp[:, 0:N], r32(negU[:]), r32(sp_[:, 0:N]), start=False, stop=False, skip_group_check=True)
            return e.matmul(zp[:, 0:N], r32(negones[:]), r32(Rr[:, c0:512]), start=False, stop=True, skip_group_check=True)
        P.op("pe", mmc, reads=[b_consts, spb, Rb, zb], writes=[zb])
        a_, ab_ = at_t[g % NA], b_att[g % NA]
        P.op("act", lambda e: e.activation(out=a_[:, 0:N], in_=zp[:, 0:N], func=AF.Exp), reads=[zb],
             writes=[ab_] + (gsm_all if g < 3 else []))

    def sbV(g):
        h, i = divmod(g, SB_NT)
        s_ = ssets[h % 2]
        qg, kb, c0, first, last = sb_tiles[i]
        N = 512 - c0
        op_, ob_ = pb[6 + qg % 2], pbb[6 + qg % 2]
        a_, ab_ = at_t[g % NA], b_att[g % NA]
        P.op("pe", lambda e: e.matmul(op_[:, c0:512], s_.sv_tm[:, kb, :], a_[:, 0:N], start=first, stop=last,
                                      skip_group_check=True),
             reads=s_.b_svtm + [ab_], writes=[ob_])
        if last:
            P.op("dve", lambda e: e.tensor_tensor(out=osb[:, h, qg * 512:(qg + 1) * 512], in0=op_[:, :],
                                                 in1=s_.sg16[:, qg * 512:(qg + 1) * 512], op=ALU.mult),
                 reads=[ob_] + s_.b_sg, writes=b_osbk[h])

    def sb_all():
        side = None
        acc = 0.0
        rate = float(SB_PRO_N) / (SB_NT - SB_DV - 4)
        for g in range(SB_G + SB_DV):
            if g < SB_G:
                sbZ(g)
            if 0 <= g - SB_DC < SB_G:
                sbC(g - SB_DC)
            if 0 <= g - SB_DV < SB_G:
                sbV(g - SB_DV)
            h, i = divmod(g, SB_NT)
            if h < 7 and i == SB_DV:
                side = sb_pro(h + 1)
                acc = 0.0
            if side is not None:
                acc += rate
                if i == SB_NT - 1:
                    acc = 1e9
                while side is not None and acc >= 1.0:
                    acc -= 1.0
                    try:
                        next(side)
                    except StopIteration:
                        side = None
    SB_MAIN_N = SB_NT - 4

    def run_all(gen):
        for _ in gen:
            pass

    def interleave(main, side, n_main, n_side):
        acc = 0.0
        for _ in main:
            if side is None:
                continue
            acc += float(n_side) / n_main
            while acc >= 1.0:
                acc -= 1.0
                try:
                    next(side)
                except StopIteration:
                    side = None
                    break
        if side is not None:
            run_all(side)

    STOP_G = stop_after == "gla"
    run_all(gla_pro(0))
    for h in range(4):
        if h < 3:
            interleave(gla_main(h), gla_pro(h + 1), GLA_MAIN_N, GLA_PRO_N)
        elif STOP_G:
            run_all(gla_main(h))
        else:
            interleave(gla_main(h), sb_pro(0), GLA_MAIN_N, SB_PRO_N)
    dump_chunks("og", og, RA[:, 0:2048])
    if STOP_G:
        return finish()
    side_banks[:] = [0, 1]
    sb_all()
    merge_pre = {}
    for c in range(2):
        merge_pre[c] = (wload(w_in, [(C_ML + c * 128, 128), (C_ML + D + c * 128, 128)]),
                        wload(w_pa, [(w_pa, c * 128, 128), (w_pb, c * 128, 128)]))
    P.barrier()
    dump_chunks("osb", osb, RA[:, 11264:11264 + 2048])
    if stop_after == "sb":
        return finish()
    b_osb = [None] * 8

    mg = RA[:, 0:8192].bitcast(BF16).rearrange("p (k t) -> p k t", k=8)
    ga_t = [RA[:, 8192 + i * 512:8192 + (i + 1) * 512] for i in range(2)]
    gb_t = [RA[:, 9216 + i * 512:9216 + (i + 1) * 512] for i in range(2)]
    ya_t = [RA[:, 10240 + i * 512:10240 + (i + 1) * 512] for i in range(2)]
    b_mg = [Buf("mg%d" % k) for k in range(8)]
    b_ga = [Buf("ga0"), Buf("ga1")]
    b_gb = [Buf("gb0"), Buf("gb1")]
    b_ya = [Buf("ya0"), Buf("ya1")]

    def mm8(e, ps, wt, off, src, ts):
        last = None
        for k in range(8):
            last = e.matmul(ps[:, :], wt[:, k, off:off + 128], src[:, k, ts], start=(k == 0), stop=(k == 7))
        return last

    osb_all = [b for k in range(8) for b in b_osbk[k]]

    def merge_pe(c, tg, par, wg, wgb, og_, wy, wyb, oy):
        ts = slice(tg * 512, (tg + 1) * 512)
        B = par * 4
        P.op("pe", lambda e: mm8(e, pb[B + 0], wg, og_[0], hT, ts), reads=[wgb] + b_hTk, writes=[pbb[B + 0]])
        P.op("pe", lambda e: mm8(e, pb[B + 1], wg, og_[1], hT, ts), reads=[wgb] + b_hTk, writes=[pbb[B + 1]])
        P.op("pe", lambda e: mm8(e, pb[B + 2], wy, oy[0], og, ts), reads=[wyb] + b_og, writes=[pbb[B + 2]])
        P.op("pe", lambda e: mm8(e, pb[B + 3], wy, oy[1], osb, ts), reads=[wyb] + osb_all, writes=[pbb[B + 3]])

    def merge_ev(c, tg, par):
        ts = slice(tg * 512, (tg + 1) * 512)
        B = par * 4
        j = par
        P.op("act", lambda e: e.activation(out=ga_t[j], in_=pb[B + 0][:, :], func=AF.Sigmoid, bias=bgate[:, c:c + 1]),
             reads=[pbb[B + 0], b_bg], writes=[b_ga[j]])
        P.op("act", lambda e: e.activation(out=gb_t[j], in_=pb[B + 1][:, :], func=AF.Sigmoid, bias=bgate[:, 8 + c:9 + c]),
             reads=[pbb[B + 1], b_bg], writes=[b_gb[j]])
        P.op("dve", lambda e: e.tensor_tensor(out=ya_t[j], in0=pb[B + 2][:, :], in1=ga_t[j], op=ALU.mult),
             reads=[pbb[B + 2], b_ga[j]], writes=[b_ya[j]])
        P.op("dve", lambda e: e.tensor_tensor(out=gb_t[j], in0=pb[B + 3][:, :], in1=gb_t[j], op=ALU.mult),
             reads=[pbb[B + 3], b_gb[j]], writes=[b_gb[j]])
        P.op("dve", lambda e: e.tensor_tensor(out=mg[:, c, ts], in0=ya_t[j], in1=gb_t[j], op=ALU.add),
             reads=[b_ya[j], b_gb[j]], writes=[b_mg[c]])

    msteps = []
    for c in range(8):
        if c in merge_pre:
            (wg, wgb, og_), (wy, wyb, oy) = merge_pre[c]
        else:
            wg, wgb, og_ = wload(w_in, [(C_ML + c * 128, 128), (C_ML + D + c * 128, 128)])
            wy, wyb, oy = wload(w_pa, [(w_pa, c * 128, 128), (w_pb, c * 128, 128)])
        for tg in range(4):
            msteps.append((c, tg, wg, wgb, og_, wy, wyb, oy))
        while len(msteps) > 0 and (len(msteps) >= 4 or c == 7):
            c_, tg_, a1, a2, a3, a4, a5, a6 = msteps.pop(0)
            sidx = c_ * 4 + tg_
            merge_pe(c_, tg_, sidx % 2, a1, a2, a3, a4, a5, a6)
            if sidx >= 1:
                merge_ev((sidx - 1) // 4, (sidx - 1) % 4, (sidx - 1) % 2)
    merge_ev(7, 3, 31 % 2)
    wo_sl = []
    for i in range(4):
        wo_sl.append(wload(w_o, [(i * 256, 256)]))
    xr = [RA[:, 11264 + i * 1024:11264 + (i + 1) * 1024] for i in range(4)]
    b_xr = [Buf("xr%d" % i) for i in range(4)]
    fing = gg1[:, 0:1024]
    junk2 = gg1[:, 1024:2048]
    b_fing = Buf("fing")
    if not (debug and "mg" in debug):
        P.dma("sp", lambda e: e.dma_start(out=fing, in_=final_g_d.partition_broadcast(128)), writes=[b_fing], sembuf=b_fing)
        for tt in range(4):
            P.dma("sp", lambda e, tt=tt: e.dma_start(out=xr[tt], in_=x_d[tt * 128:(tt + 1) * 128, :]), writes=[b_xr[tt]], sembuf=b_xr[tt])
    P.barrier()
    dump_chunks("mg", mg, RA[:, 12288:12288 + 2048])
    if stop_after == "merge":
        return finish()

    HF = hT[:].rearrange("p k t -> p (k t)").bitcast(F32)
    yo = [HF[:, 4096 + i * 1024:4096 + (i + 1) * 1024] for i in range(4)]
    b_yo = [Buf("yo%d" % i) for i in range(4)]
    b_sm2 = [Buf("sm2_%d" % i) for i in range(4)]
    b_junk2 = Buf("junk2")

    def f_dma(tt):
        j = tt % 4
        P.dma("sp", lambda e: e.dma_start(out=xr[j], in_=x_d[tt * 128:(tt + 1) * 128, :]), writes=[b_xr[j]], sembuf=b_xr[j])

    def f_A(tt):
        j = tt % 4
        xi, xb = xr[j], b_xr[j]
        sm = small[:, 16 + j * 4:16 + j * 4 + 4]
        smb = b_sm2[j]
        for half in range(2):
            ps, psb = pb[(tt * 2 + half) % 8], pbb[(tt * 2 + half) % 8]

            def mmo(e, ps=ps, half=half):
                last = None
                for q in range(2):
                    wt = wo_sl[half * 2 + q][0]
                    for k in range(8):
                        last = e.matmul(ps[:, q * 256:(q + 1) * 256], mg[:, k, tt * 128:(tt + 1) * 128], wt[:, k, 0:256],
                                        start=(k == 0), stop=(k == 7))
                return last
            P.op("pe", mmo, reads=b_mg + [wo_sl[half * 2][1], wo_sl[half * 2 + 1][1]], writes=[psb])
            P.op("dve", lambda e, ps=ps, half=half: e.tensor_tensor(out=xi[:, half * 512:(half + 1) * 512], in0=ps[:, :],
                                                                 in1=xi[:, half * 512:(half + 1) * 512], op=ALU.add),
                 reads=[psb], writes=[xb])
        P.op("act", lambda e: e.activation(out=junk2, in_=xi, func=AF.Square, accum_out=sm[:, 0:1]), reads=[xb], writes=[smb])
        P.op("act", lambda e: e.activation(out=sm[:, 1:2], in_=sm[:, 0:1], func=AF.Ln, scale=1.0 / D, bias=EPS),
             reads=[smb], writes=[smb])
        P.op("act", lambda e: e.activation(out=sm[:, 2:3], in_=sm[:, 1:2], func=AF.Exp, scale=-0.5), reads=[smb], writes=[smb])

    def f_B(tt):
        j = tt % 4
        xi, xb = xr[j], b_xr[j]
        sm = small[:, 16 + j * 4:16 + j * 4 + 4]
        smb = b_sm2[j]
        yi, yb = yo[j], b_yo[j]
        P.op("dve", lambda e: e.scalar_tensor_tensor(out=yi, in0=xi, scalar=sm[:, 2:3], in1=fing, op0=ALU.mult, op1=ALU.mult),
             reads=[xb, smb, b_fing], writes=[yb])
        out_toks.append(P.dma("sp", lambda e: e.dma_start(out=out_d[tt * 128:(tt + 1) * 128, :], in_=yi),
                              reads=[yb], sembuf=yb))

    for step in range(NT + 1):
        if step < NT:
            f_A(step)
        if step >= 1:
            f_B(step - 1)
            if step - 1 + 4 < NT:
                f_dma(step - 1 + 4)
    return finish()


_NC_CACHE = {}


def kernel(x, norm_g, w_in, w_dec_up, b_dec, gla_norm_g, w_pa, w_pb, b_gate, w_o, final_g):
    n = 8
    if "nc" not in _NC_CACHE:
        _NC_CACHE["nc"] = build_nc()
    nc = _NC_CACHE["nc"]
    cst = host_consts()
    f = lambda a: np.ascontiguousarray(np.asarray(a, dtype=np.float32))
    shared = {"norm_g": f(norm_g), "w_in": f(w_in), "w_dec_up": f(w_dec_up), "b_dec": f(b_dec),
              "gla_norm_g": f(gla_norm_g), "w_pa": f(w_pa), "w_pb": f(w_pb), "b_gate": f(b_gate),
              "w_o": f(w_o), "final_g": f(final_g), "cst": cst}
    xs = f(x)
    in_maps = [dict(shared, x=xs[b]) for b in range(n)]
    res = run_bass_kernel_spmd(nc, in_maps, core_ids=list(range(n)))
    return np.stack([r["out"] for r in res.results], axis=0)
```
